# Optimizing a Trainium2 kernel written in Bass

```python
import math
import jax, jax.numpy as jnp
from jax import lax
import numpy as np

D_MODEL = 1024
BATCH = 8
SEQ = 2048
DEPTH = 1
DEC_BATCH = 128
DEC_SEQ = 8
PAST_LEN = 16384
PAGE_SIZE = 128

EPS = 1e-5
W_MIX = D_MODEL
W_CONV = W_MIX // 2
W_SSM = W_MIX - W_CONV
CONV_W = 3
SSM_H = 16
N_GROUPS = W_SSM // SSM_H
STATE_P = 64
D_FF = ((8 * D_MODEL // 3 + 255) // 256) * 256

kernel_name = 'hybrid_shortconv_s5_decode_step'


def rms_norm(x, g):
    xf = x.astype(jnp.float32)
    y = xf * lax.rsqrt(jnp.mean(xf * xf, axis=-1, keepdims=True) + EPS)
    return (y * g.astype(jnp.float32)).astype(x.dtype)


def short_conv_group(b_g, c_g, xc, conv_past, conv_w):
    L = xc.shape[1]
    v = c_g * xc
    v_ext = jnp.concatenate([conv_past.astype(v.dtype), v], axis=1)
    conv = sum(conv_w[k] * v_ext[:, k:k + L] for k in range(CONV_W))
    return b_g * conv, v_ext[:, L:]


def _ssm_combine(left, right):
    a_l, b_l = left
    a_r, b_r = right
    return a_r * a_l, a_r * b_l + b_r


def s5_group(u, h0_re, h0_im, lam_re, lam_im, log_dt, b_re, b_im, c_re, c_im, d_skip, w_glu, b_glu):
    f32 = jnp.float32
    bsz, L, _ = u.shape
    uf = u.astype(f32).reshape(bsz, L, N_GROUPS, SSM_H)
    lam = lax.complex(lam_re.astype(f32), lam_im.astype(f32))
    dt = jnp.exp(log_dt.astype(f32))[:, None]
    lam_bar = jnp.exp(lam * dt)
    b_mat = lax.complex(b_re.astype(f32), b_im.astype(f32))
    b_bar = ((lam_bar - 1.0) / lam)[..., None] * b_mat
    c_mat = lax.complex(c_re.astype(f32), c_im.astype(f32))
    bu = jnp.einsum('gph,blgh->blgp', b_bar, uf.astype(jnp.complex64))
    h0 = lax.complex(h0_re.astype(f32), h0_im.astype(f32))
    bu = bu.at[:, 0].add(lam_bar * h0)
    a = jnp.broadcast_to(lam_bar, bu.shape)
    _, h = lax.associative_scan(_ssm_combine, (a, bu), axis=1)
    y = jnp.einsum('ghp,blgp->blgh', c_mat, h).real + d_skip.astype(f32).reshape(N_GROUPS, SSM_H) * uf
    z = jax.nn.gelu(y.reshape(bsz, L, W_SSM))
    gate = jax.nn.sigmoid(z @ w_glu.astype(f32) + b_glu.astype(f32))
    out = (z * gate).astype(u.dtype)
    h_last = h[:, -1]
    return out, jnp.real(h_last).astype(h0_re.dtype), jnp.imag(h_last).astype(h0_re.dtype)


def hybrid_layer(x, conv_past, h_re, h_im, norm_mix, w_in, conv_w, lam_re, lam_im, log_dt,
                 b_re, b_im, c_re, c_im, d_skip, w_glu, b_glu, w_out,
                 norm_ffn, w_gate, w_up, w_down):
    xn = rms_norm(x, norm_mix)
    proj = xn @ w_in
    b_g, c_g, xc, u = jnp.split(proj, [W_CONV, 2 * W_CONV, 3 * W_CONV], axis=-1)
    conv_out, conv_new = short_conv_group(b_g, c_g, xc, conv_past, conv_w)
    ssm_out, hr, hi = s5_group(u, h_re, h_im, lam_re, lam_im, log_dt, b_re, b_im,
                               c_re, c_im, d_skip, w_glu, b_glu)
    x = x + jnp.concatenate([conv_out, ssm_out], axis=-1) @ w_out
    hn = rms_norm(x, norm_ffn)
    x = x + (jax.nn.silu(hn @ w_gate) * (hn @ w_up)) @ w_down
    return x, conv_new, hr, hi


def setup_inputs(seed: int = 0) -> dict:
    key = jax.random.key(seed)
    ks = jax.random.split(key, 26)
    nrm = jax.random.normal
    lam_im_base = math.pi * jnp.arange(STATE_P, dtype=jnp.float32)
    return {
        'x_prompt': nrm(ks[0], (BATCH, SEQ, D_MODEL), jnp.float32),
        'x_sample': nrm(ks[1], (DEC_BATCH, DEC_SEQ, D_MODEL), jnp.float32),
        'cache_conv': nrm(ks[2], (DEPTH, DEC_BATCH, CONV_W - 1, W_CONV), jnp.float32),
        'state_ssm_re': 0.5 * nrm(ks[3], (DEPTH, DEC_BATCH, N_GROUPS, STATE_P), jnp.float32),
        'state_ssm_im': 0.5 * nrm(ks[4], (DEPTH, DEC_BATCH, N_GROUPS, STATE_P), jnp.float32),
        'norm_mix': 1.0 + 0.02 * nrm(ks[5], (DEPTH, D_MODEL), jnp.float32),
        'w_in': nrm(ks[6], (DEPTH, D_MODEL, 3 * W_CONV + W_SSM), jnp.float32) * D_MODEL ** -0.5,
        'conv_w': nrm(ks[7], (DEPTH, CONV_W, W_CONV), jnp.float32) * CONV_W ** -0.5,
        'ssm_lam_re': -0.5 + 0.01 * nrm(ks[8], (DEPTH, N_GROUPS, STATE_P), jnp.float32),
        'ssm_lam_im': lam_im_base + 0.01 * nrm(ks[9], (DEPTH, N_GROUPS, STATE_P), jnp.float32),
        'ssm_log_dt': jax.random.uniform(ks[10], (DEPTH, N_GROUPS), jnp.float32,
                                         minval=math.log(1e-3), maxval=math.log(1e-1)),
        'ssm_b_re': nrm(ks[11], (DEPTH, N_GROUPS, STATE_P, SSM_H), jnp.float32) * (2 * SSM_H) ** -0.5,
        'ssm_b_im': nrm(ks[12], (DEPTH, N_GROUPS, STATE_P, SSM_H), jnp.float32) * (2 * SSM_H) ** -0.5,
        'ssm_c_re': nrm(ks[13], (DEPTH, N_GROUPS, SSM_H, STATE_P), jnp.float32) * STATE_P ** -0.5,
        'ssm_c_im': nrm(ks[14], (DEPTH, N_GROUPS, SSM_H, STATE_P), jnp.float32) * STATE_P ** -0.5,
        'ssm_d': nrm(ks[15], (DEPTH, W_SSM), jnp.float32),
        'w_glu': nrm(ks[16], (DEPTH, W_SSM, W_SSM), jnp.float32) * W_SSM ** -0.5,
        'b_glu': 0.01 * nrm(ks[17], (DEPTH, W_SSM), jnp.float32),
        'w_out': nrm(ks[18], (DEPTH, W_MIX, D_MODEL), jnp.float32) * W_MIX ** -0.5,
        'norm_ffn': 1.0 + 0.02 * nrm(ks[19], (DEPTH, D_MODEL), jnp.float32),
        'w_gate': nrm(ks[20], (DEPTH, D_MODEL, D_FF), jnp.float32) * D_MODEL ** -0.5,
        'w_up': nrm(ks[21], (DEPTH, D_MODEL, D_FF), jnp.float32) * D_MODEL ** -0.5,
        'w_down': nrm(ks[22], (DEPTH, D_FF, D_MODEL), jnp.float32) * D_FF ** -0.5,
        'norm_final': 1.0 + 0.02 * nrm(ks[23], (D_MODEL,), jnp.float32),
    }


def reference(x_prompt, x_sample, cache_conv, state_ssm_re, state_ssm_im,
              norm_mix, w_in, conv_w, ssm_lam_re, ssm_lam_im, ssm_log_dt,
              ssm_b_re, ssm_b_im, ssm_c_re, ssm_c_im, ssm_d, w_glu, b_glu, w_out,
              norm_ffn, w_gate, w_up, w_down, norm_final):
    bp = x_prompt.shape[0]
    xp, xs = x_prompt, x_sample
    conv_p, re_p, im_p, conv_s, re_s, im_s = [], [], [], [], [], []
    for l in range(DEPTH):
        weights = (norm_mix[l], w_in[l], conv_w[l], ssm_lam_re[l], ssm_lam_im[l], ssm_log_dt[l],
                   ssm_b_re[l], ssm_b_im[l], ssm_c_re[l], ssm_c_im[l], ssm_d[l], w_glu[l], b_glu[l],
                   w_out[l], norm_ffn[l], w_gate[l], w_up[l], w_down[l])
        zc = jnp.zeros((bp, CONV_W - 1, W_CONV), xp.dtype)
        zh = jnp.zeros((bp, N_GROUPS, STATE_P), state_ssm_re.dtype)
        xp, c_new, hr, hi = hybrid_layer(xp, zc, zh, zh, *weights)
        conv_p.append(c_new); re_p.append(hr); im_p.append(hi)
        xs, c_new, hr, hi = hybrid_layer(xs, cache_conv[l], state_ssm_re[l], state_ssm_im[l], *weights)
        conv_s.append(c_new); re_s.append(hr); im_s.append(hi)
    y_prompt = rms_norm(xp, norm_final)
    y_sample = rms_norm(xs, norm_final)
    return (y_prompt, y_sample,
            jnp.stack(conv_p), jnp.stack(re_p), jnp.stack(im_p),
            jnp.stack(conv_s), jnp.stack(re_s), jnp.stack(im_s))
```

```python
import math
from contextlib import ExitStack

import numpy as np
import concourse.bass as bass
import concourse.mybir as mybir
from concourse.bass_utils import run_bass_kernel_spmd

F32 = mybir.dt.float32
BF16 = mybir.dt.bfloat16
AF = mybir.ActivationFunctionType
ALU = mybir.AluOpType

ENGS = ("pe", "act", "dve", "pool", "sp")
NCORES = 8
D = 1024
DFF = 2816
NF = 22
EPS = 1e-5
PI = math.pi
TWO_PI = 2.0 * math.pi


class Buf:
    __slots__ = ("name", "w", "r")

    def __init__(self, name):
        self.name = name
        self.w = None
        self.r = {}


class Chan:
    def __init__(self, key):
        self.key = key
        self.count = 0


class Prog:
    def __init__(self, nc, stack):
        self.nc = nc
        self.stack = stack
        self.ops = {e: [] for e in ENGS}
        self.cnt = {e: 0 for e in ENGS}
        self.waited = {e: {} for e in ENGS}
        self.sem = {}
        for e in ENGS:
            self.sem[e] = stack.enter_context(nc.semaphore("sem_" + e))
        self.nchan = 0
        self.finals = []
        self.final_chans = {}

    def chan(self, final=False):
        key = "ch%d" % self.nchan
        self.nchan += 1
        self.sem[key] = self.stack.enter_context(self.nc.semaphore("sem_" + key))
        c = Chan(key)
        if final:
            self.final_chans[key] = c
        return c

    def emit(self, eng, fn, reads=(), writes=(), chan=None, signal=True):
        deps = {}

        def add(k, v, src):
            if k not in deps or deps[k][0] < v:
                deps[k] = (v, src)

        raw = {}
        for b in reads:
            if b.w is not None:
                add(*b.w)
                k_, v_, s_ = b.w
                if k_ not in raw or raw[k_] < v_:
                    raw[k_] = v_
        for b in writes:
            if b.w is not None:
                add(*b.w)
            for k, (v, src) in b.r.items():
                add(k, v, src)
        waits = []
        for k, (v, src) in deps.items():
            if src == "pe" and eng == "pe":
                continue
            if chan is not None and k == chan.key and k in self.final_chans:
                continue
            if self.waited[eng].get(k, 0) >= v:
                continue
            self.waited[eng][k] = v
            waits.append((k, v))
        if chan is not None:
            chan.count += 16
            tok = (chan.key, chan.count, "dma")
            inc = (chan.key, 16)
        elif signal:
            self.cnt[eng] += 1
            tok = (eng, self.cnt[eng], eng)
            inc = (eng, 1)
        else:
            tok = (eng, self.cnt[eng] + 1, eng)
            inc = None
        for b in reads:
            k, v, src = tok
            if k not in b.r or b.r[k][0] < v:
                b.r[k] = (v, src)
        for b in writes:
            b.w = tok
            b.r = {}
        self.ops[eng].append((waits, fn, inc))
        return tok

    def final_wait(self, eng, chan):
        self.finals.append((eng, chan))

    def replay(self):
        nc = self.nc
        with nc.Block() as block:
            def run(engname):
                def body(e):
                    for waits, fn, inc in self.ops[engname]:
                        for k, v in waits:
                            if k in self.final_chans:
                                v = self.final_chans[k].count
                            e.wait_ge(self.sem[k], v)
                        ins = fn(e)
                        if inc is not None:
                            ins.then_inc(self.sem[inc[0]], inc[1])
                    for en, ch in self.finals:
                        if en == engname and ch.count > 0:
                            e.wait_ge(self.sem[ch.key], ch.count)
                return body

            block.tensor(run("pe"))
            block.scalar(run("act"))
            block.vector(run("dve"))
            block.gpsimd(run("pool"))
            block.sync(run("sp"))


def build_nc():
    nc = bass.Bass("TRN2", target_bir_lowering=False)
    dt_in = lambda n, s: nc.dram_tensor(n, s, F32, kind="ExternalInput")
    dt_out = lambda n, s: nc.dram_tensor(n, s, F32, kind="ExternalOutput")
    xp = dt_in("xp", [2048, D]).ap()
    xs = dt_in("xs", [128, D]).ap()
    cc = dt_in("cc", [16, 1024]).ap()
    h0r = dt_in("h0r", [16, 2048]).ap()
    h0i = dt_in("h0i", [16, 2048]).ap()
    g1_d = dt_in("g1", [D]).ap()
    g2_d = dt_in("g2", [D]).ap()
    g3_d = dt_in("g3", [D]).ap()
    w_in_d = dt_in("w_in", [D, 2048]).ap()
    convw_d = dt_in("convw", [3, 512])
    lamr_d = dt_in("lamr", [16, 128]).ap()
    lami_d = dt_in("lami", [16, 128]).ap()
    ldt_d = dt_in("ldt", [16, 2])
    br_d = dt_in("br", [32, 64, 16])
    bi_d = dt_in("bi", [32, 64, 16])
    cr_d = dt_in("cr", [32, 16, 64])
    ci_d = dt_in("ci", [32, 16, 64])
    dsk_d = dt_in("dsk", [512]).ap()
    wglu_d = dt_in("wglu", [512, 512]).ap()
    bglu_d = dt_in("bglu", [512]).ap()
    wout_d = dt_in("wout", [D, D]).ap()
    wg_d = dt_in("wg", [D, DFF]).ap()
    wu_d = dt_in("wu", [D, DFF]).ap()
    wd_d = dt_in("wd", [DFF, D]).ap()

    yp = dt_out("yp", [2048, D]).ap()
    ys = dt_out("ys", [128, D]).ap()
    convp_o = dt_out("convp", [2, 512]).ap()
    rep_o = dt_out("rep", [16, 128]).ap()
    imp_o = dt_out("imp", [16, 128]).ap()
    convs_o = dt_out("convs", [16, 1024]).ap()
    res_o = dt_out("res", [16, 2048]).ap()
    ims_o = dt_out("ims", [16, 2048]).ap()

    win_s = nc.dram_tensor("win_s", [16, 128, 8, 128], BF16, kind="Internal").ap()
    wgu_s = nc.dram_tensor("wgu_s", [NF, 128, 2, 8, 128], BF16, kind="Internal").ap()
    wd_s = nc.dram_tensor("wd_s", [NF, 128, D], BF16, kind="Internal").ap()
    wout_s = nc.dram_tensor("wout_s", [8, 128, D], BF16, kind="Internal").ap()

    with ExitStack() as st, nc.allow_non_contiguous_dma(reason="small param layouts"):
        P = Prog(nc, st)
        sb = lambda n, s, d=F32: st.enter_context(nc.sbuf_tensor(n, s, d))
        E = P.emit

        banks = [st.enter_context(nc.psum_tensor("bank%d" % i, [128, 512], F32)) for i in range(8)]
        bbufs = [Buf("bank%d" % i) for i in range(8)]
        bank_i = [0]

        pinned = set()

        def next_bank(pin=False):
            while True:
                i = bank_i[0] % 8
                bank_i[0] += 1
                if i not in pinned:
                    break
            if pin:
                pinned.add(i)
            return banks[i], bbufs[i]

        def unpin(bbuf):
            pinned.discard(bbufs.index(bbuf))

        XB = sb("XB", [128, 4, D]); b_xb = [Buf("xb%d" % t) for t in range(4)]
        SETUPT = sb("SETUPT", [128, 6656])
        ARENA = sb("ARENA", [128, 5632])
        STG = [sb("STG%d" % i, [128, 1024]) for i in range(3)]
        b_stgs = [Buf("stg%d" % i) for i in range(3)]
        OSC = sb("OSC", [16, 1024]); b_osc = Buf("OSC")
        CCN = OSC

        def handover(olds, news):
            for n in news:
                for o in olds:
                    if o.w is not None and n.r.get(o.w[0], (0, ""))[0] < o.w[1]:
                        n.r[o.w[0]] = (o.w[1], o.w[2])
                    for k_, (v_, s_) in o.r.items():
                        if n.r.get(k_, (0, ""))[0] < v_:
                            n.r[k_] = (v_, s_)

        xin_ch = [P.chan() for _ in range(8)]
        E("sp", lambda e: e.dma_start(out=XB[:, 0, :], in_=xs), writes=[b_xb[0]], chan=xin_ch[0])

        ld = P.chan(final=True)
        ldm = P.chan(final=True)
        ldc = P.chan(final=True)
        ones = sb("ones", [128, 128]); b_ones = Buf("ones")
        identf = sb("identf", [128, 128]); b_identf = Buf("identf")
        identb = sb("identb", [128, 128], BF16); b_identb = Buf("identb")
        E("pool", lambda e: e.memset(ones[:], 1.0), writes=[b_ones])
        E("pool", lambda e: e.affine_select(out=identf[:], in_=ones[:], pattern=[[-1, 128]],
                                            compare_op=ALU.is_equal, fill=0.0, base=0,
                                            channel_multiplier=1), reads=[b_ones], writes=[b_identf])
        E("pool", lambda e: e.tensor_copy(out=identb[:], in_=identf[:]), reads=[b_identf], writes=[b_identb])

        G3 = sb("G3", [128, D]); G1C = sb("G1C", [128, 8]); G2C = sb("G2C", [128, 8])
        b_G = Buf("G")
        CW = sb("CW", [128, 4, 3]); DV = sb("DV", [128, 4]); BG = sb("BG", [128, 4])
        LDT = sb("LDT", [128, 16])
        BR = sb("BR", [128, 16, 16]); BI = sb("BI", [128, 16, 16])
        b_par = Buf("params")
        b_parm = Buf("params_m")
        b_cst = Buf("CST")
        b_tmp = Buf("ssm_tmp")
        b_bd = Buf("BD")
        b_bl = [Buf("chain0"), Buf("chain1")]
        ldg = P.chan(final=True)
        E("sp", lambda e: e.dma_start(out=G1C[:], in_=g1_d.rearrange("(k p) -> p k", p=128)), writes=[b_G], chan=ldg)
        E("sp", lambda e: e.dma_start(out=G2C[:], in_=g2_d.rearrange("(k p) -> p k", p=128)), writes=[b_G], chan=ldg)
        E("sp", lambda e: e.dma_start(out=G3[:], in_=g3_d.partition_broadcast(128)), writes=[b_G], chan=ldg)
        for kk in range(3):
            E("pool", lambda e, kk=kk: e.dma_start(out=CW[:, :, kk], in_=bass.AP(convw_d, kk * 512, [[1, 128], [128, 4]])),
              writes=[b_parm], chan=ldm)
        E("pool", lambda e: e.dma_start(out=DV[:], in_=dsk_d.rearrange("(j p) -> p j", p=128)), writes=[b_parm], chan=ldm)
        E("pool", lambda e: e.dma_start(out=BG[:], in_=bglu_d.rearrange("(j p) -> p j", p=128)), writes=[b_parm], chan=ldm)
        PARN = sb("PARN", [32, 128]); b_parn = Buf("PARN")
        LRLI = sb("LRLI", [128, 32]); b_lr = Buf("LRLI")
        LR = LRLI[:, 0:16]; LI = LRLI[:, 16:32]
        b_ldt = Buf("LDT")
        ldl = P.chan(final=True)
        E("sp", lambda e: e.dma_start(out=PARN[0:16, :], in_=lamr_d), writes=[b_parn], chan=ldl)
        E("sp", lambda e: e.dma_start(out=PARN[16:32, :], in_=lami_d), writes=[b_parn], chan=ldl)
        ldt_ch = P.chan(final=True)
        for ee in range(2):
            E("sp", lambda e, ee=ee: e.dma_start(out=LDT[ee * 64:(ee + 1) * 64, :],
                                               in_=bass.AP(ldt_d, ee, [[0, 64], [2, 16]])),
              writes=[b_ldt], chan=ldt_ch)
        bk_, bb_ = next_bank()
        E("pe", lambda e: e.matmul(bk_[:, 0:32], lhsT=PARN[:, :], rhs=identf[0:32, 0:32], start=True, stop=True),
          reads=[b_parn, b_identf], writes=[bb_])
        E("dve", lambda e: e.tensor_copy(out=LRLI[:, :], in_=bk_[:, 0:32]), reads=[bb_], writes=[b_lr])
        for ee in range(2):
            E("act", lambda e, ee=ee: e.dma_start(out=BR[ee * 64:(ee + 1) * 64, :, :],
                                                 in_=bass.AP(br_d, ee * 1024, [[16, 64], [2048, 16], [1, 16]])),
              writes=[b_par], chan=ld)
            E("act", lambda e, ee=ee: e.dma_start(out=BI[ee * 64:(ee + 1) * 64, :, :],
                                                 in_=bass.AP(bi_d, ee * 1024, [[16, 64], [2048, 16], [1, 16]])),
              writes=[b_par], chan=ld)

        b_ssm = Buf("ssmprep")
        sv = lambda a, b_: SETUPT[:, a:b_]
        v16 = lambda a: SETUPT[:, a:a + 256].rearrange("p (a h) -> p a h", h=16)
        v32 = lambda a: SETUPT[:, a:a + 512].rearrange("p (a h) -> p a h", h=32)
        BDR = v32(0); NBDI = v32(512)
        CHN = [(v32(1024), v32(1536)), (v32(2048), v32(2560))]
        BBR = v16(3072); BBI = v16(3328); CRT = v16(3584); CIT = v16(3840)
        K0F = SETUPT[:, 4096:4608].rearrange("p (j c) -> p j c", c=128)
        TA = v32(4608); TB = v32(5120); TC = v32(5632); TD = v32(6144)
        U0 = v16(6144); U1 = v16(6400)
        CST = SETUPT[0:64, 4608:5632].rearrange("p (r j c) -> p r j c", r=2, j=4)
        for ri, cd in enumerate((cr_d, ci_d)):
            for j in range(4):
                for k in range(4):
                    pr = 4 * j + k
                    E("pool", lambda e, ri=ri, cd=cd, j=j, k=k, pr=pr: e.dma_start(
                        out=CST[k * 16:(k + 1) * 16, ri, j, :].rearrange("p (e q) -> p e q", e=2),
                        in_=bass.AP(cd, 2 * pr * 1024, [[64, 16], [1024, 2], [1, 64]])),
                      writes=[b_cst], chan=ldc)
        E("pool", lambda e: e.dma_start(out=CCN[:], in_=cc), writes=[b_osc], chan=ldm)

        WGLU = sb("WGLU", [128, 4, 512], BF16); b_wglu = Buf("WGLU")
        stg_ch = [P.chan() for _ in range(3)]
        stg_i = [0]
        cast_i = [0]
        scr_bufs = {}

        def stage_cast(dst_ap, src_ap, gain, wr_bufs, force_act=False, kbufs=None):
            i = stg_i[0] % 3
            stg_i[0] += 1
            st_t = STG[i]
            if len(src_ap.shape) == 3:
                view = st_t[:, 0:src_ap.shape[1] * src_ap.shape[2]].rearrange("p (k c) -> p k c", c=src_ap.shape[2])
            else:
                view = st_t[:, 0:src_ap.shape[1]]
            E("sp", lambda e: e.dma_start(out=view, in_=src_ap), writes=[b_stgs[i]], chan=stg_ch[i])
            on_act = force_act or (cast_i[0] % 2 == 0)
            cast_i[0] += 1
            rd = [b_stgs[i], b_G]
            if gain is None:
                if on_act:
                    E("act", lambda e: e.copy(out=dst_ap, in_=view), reads=rd, writes=wr_bufs)
                else:
                    E("dve", lambda e: e.tensor_copy(out=dst_ap, in_=view), reads=rd, writes=wr_bufs)
            else:
                if on_act:
                    for k in range(8):
                        E("act", lambda e, k=k: e.mul(out=dst_ap[:, k, :], in_=view[:, k, :], mul=gain[:, k:k + 1]),
                          reads=rd, writes=([kbufs[k]] if kbufs is not None else wr_bufs))
                else:
                    E("dve", lambda e: e.tensor_tensor(out=dst_ap, in0=view,
                                                       in1=gain.unsqueeze(2).to_broadcast([128, 8, 128]), op=ALU.mult),
                      reads=rd, writes=(kbufs if kbufs is not None else wr_bufs))

        class WStream:
            def __init__(self, name, n, shape):
                self.name = name
                self.n = n
                self.t = [sb("%s%d" % (name, i), shape, BF16) for i in range(n)]
                nsub = 16 if name == "RGU" else (8 if name == "RWIN" else 1)
                self.b = [[Buf("%s%d_%d" % (name, i, q)) for q in range(nsub)] for i in range(n)]
                self.ch = [P.chan() for _ in range(n)]
                self.sch = [P.chan() for _ in range(n)]
                self.i = 0

            def get(self, first, key, parts_fn, scratch_ap):
                i = self.i % self.n
                self.i += 1
                t = self.t[i]
                if first:
                    for pi, (dst_ap, src_ap, gain) in enumerate(parts_fn(t)):
                        kb = self.b[i][pi * 8:(pi + 1) * 8] if gain is not None else None
                        stage_cast(dst_ap, src_ap, gain, self.b[i], force_act=(self.name == "RWIN"), kbufs=kb)
                    sbuf = Buf("scr_%s_%s" % (self.name, key))
                    scr_bufs[(self.name, key)] = sbuf
                    E("pool", lambda e: e.dma_start(out=scratch_ap, in_=t[:]), reads=self.b[i], writes=[sbuf],
                      chan=self.sch[i])
                else:
                    E("sp", lambda e: e.dma_start(out=t[:], in_=scratch_ap), reads=[scr_bufs[(self.name, key)]],
                      writes=self.b[i], chan=self.ch[i])
                return t, self.b[i]


        sm = lambda n, s: sb(n, s)
        DT = sm("DT", [128, 16]); AA = sm("AA", [128, 16]); TH = sm("TH", [128, 16]); MAG = sm("MAG", [128, 16])
        T0 = sm("T0", [128, 16]); T1 = sm("T1", [128, 16]); T2 = sm("T2", [128, 16])
        QR = sm("QR", [128, 16]); QI = sm("QI", [128, 16])
        L1R = sm("L1R", [128, 16]); L1I = sm("L1I", [128, 16])
        L8R = sm("L8R", [128, 16]); L8I = sm("L8I", [128, 16])
        RR = sm("RR", [128, 16]); TH8 = sm("TH8", [128, 16])
        COST = sm("COST", [128, 16, 64]); SINT = sm("SINT", [128, 16, 64]); RTAB = sm("RTAB", [128, 16, 64])
        IOT = sm("IOT", [128, 64])
        L2T = sb("L2T", [128, 3, 2, 256])
        b_l2 = Buf("l2")
        L2F = L2T[:, :, :, :].rearrange("p a b c -> p (a b c)")
        TI = st.enter_context(nc.sbuf_tensor("TI", [128, 16], mybir.dt.int32))
        TIB = st.enter_context(nc.sbuf_tensor("TIB", [128, 512], mybir.dt.int32))
        TF = sm("TF", [128, 16]); TY = sm("TY", [128, 16]); TM = sm("TM", [128, 16])
        KT = sb("KT", [128, 4, 8, 128], BF16)
        BLT = sb("BLT", [128, 4, 8, 2, 128], BF16)
        CH = sb("CH", [128, 16, 8, 2, 32], BF16)
        b_kt = Buf("KT"); b_blt = Buf("BLT"); b_ch = Buf("CH")

        deferred = [None]

        def V(fn, eng="dve", extra=()):
            if deferred[0] is not None:
                deferred[0].append(lambda: E(eng, fn, reads=[b_ssm, b_lr] + list(extra), writes=[b_ssm] + list(extra)))
            else:
                E(eng, fn, reads=[b_ssm, b_lr] + list(extra), writes=[b_ssm] + list(extra))

        def tt(o, a, b, op, eng="dve", extra=()):
            V(lambda e: e.tensor_tensor(out=o, in0=a, in1=b, op=op), eng, extra)

        def ts(o, a, s1, op0, s2=None, op1=None, eng="dve", extra=()):
            if op1 is None:
                V(lambda e: e.tensor_scalar(out=o, in0=a, scalar1=s1, scalar2=None, op0=op0), eng, extra)
            else:
                V(lambda e: e.tensor_scalar(out=o, in0=a, scalar1=s1, scalar2=s2, op0=op0, op1=op1), eng, extra)

        def act(o, a, func, scale=1.0, extra=()):
            V(lambda e: e.activation(out=o, in_=a, func=func, scale=scale), "act", extra)

        def range_reduce(dst, src, offset, y, ti, tf, tm, extra=()):
            ts(y, src, offset, ALU.add, extra=extra)
            ts(tf, y, 1.0 / TWO_PI, ALU.mult, extra=extra)
            V(lambda e: e.tensor_copy(out=ti, in_=tf), extra=extra)
            V(lambda e: e.tensor_copy(out=tf, in_=ti), extra=extra)
            V(lambda e: e.scalar_tensor_tensor(out=dst, in0=tf, scalar=-TWO_PI, in1=y, op0=ALU.mult, op1=ALU.add),
              extra=extra)
            ts(tm, dst, PI, ALU.is_gt, TWO_PI, ALU.mult, extra=extra)
            tt(dst, dst, tm, ALU.subtract, extra=extra)
            ts(tm, dst, -PI, ALU.is_lt, TWO_PI, ALU.mult, extra=extra)
            tt(dst, dst, tm, ALU.add, extra=extra)

        def Vx(fn, rd, wr, eng="dve"):
            E(eng, fn, reads=rd, writes=wr)

        def cmul(dst, src, cr, ci, b_dst, b_src):
            (dr, di), (sr, si) = dst, src
            for o_, a_, c_ in ((TA, sr, cr), (TB, si, ci), (TC, sr, ci), (TD, si, cr)):
                Vx(lambda e, o_=o_, a_=a_, c_=c_: e.tensor_tensor(out=o_, in0=a_, in1=c_, op=ALU.mult),
                   [b_src, b_ssm], [b_tmp])
            Vx(lambda e: e.tensor_tensor(out=dr, in0=TA, in1=TB, op=ALU.subtract), [b_tmp], [b_dst])
            Vx(lambda e: e.tensor_tensor(out=di, in0=TC, in1=TD, op=ALU.add), [b_tmp], [b_dst])

        def prep_part1():
            act(DT[:], LDT[:], AF.Exp, extra=[b_ldt])
            tt(AA[:], LR, DT[:], ALU.mult)
            tt(TH[:], LI, DT[:], ALU.mult)
            act(MAG[:], AA[:], AF.Exp)
            range_reduce(T0[:], TH[:], 0.0, TY[:], TI[:], TF[:], TM[:])
            act(T1[:], T0[:], AF.Sin)
            range_reduce(T0[:], TH[:], 0.5 * PI, TY[:], TI[:], TF[:], TM[:])
            act(T2[:], T0[:], AF.Sin)
            tt(L1R[:], MAG[:], T2[:], ALU.mult)
            tt(L1I[:], MAG[:], T1[:], ALU.mult)
            ts(T0[:], L1R[:], -1.0, ALU.add)
            tt(T1[:], LR, LR, ALU.mult)
            tt(T2[:], LI, LI, ALU.mult)
            tt(T1[:], T1[:], T2[:], ALU.add)
            V(lambda e: e.reciprocal(out=T1[:], in_=T1[:]))
            tt(QR[:], T0[:], LR, ALU.mult)
            tt(T2[:], L1I[:], LI, ALU.mult)
            tt(QR[:], QR[:], T2[:], ALU.add)
            tt(QR[:], QR[:], T1[:], ALU.mult)
            tt(QI[:], L1I[:], LR, ALU.mult)
            tt(T2[:], T0[:], LI, ALU.mult)
            tt(QI[:], QI[:], T2[:], ALU.subtract)
            tt(QI[:], QI[:], T1[:], ALU.mult)
            bc = lambda t: t.unsqueeze(2).to_broadcast([128, 16, 16])
            x_ = [b_tmp, b_par]
            tt(U0, BR[:], bc(QR[:]), ALU.mult, extra=x_)
            tt(U1, BI[:], bc(QI[:]), ALU.mult, extra=x_)
            tt(BBR, U0, U1, ALU.subtract, extra=x_)
            tt(U0, BI[:], bc(QR[:]), ALU.mult, extra=x_)
            tt(U1, BR[:], bc(QI[:]), ALU.mult, extra=x_)
            tt(BBI, U0, U1, ALU.add, extra=x_)
            (r0, i0) = CHN[0]
            Vx(lambda e: e.memset(BDR, 0.0), [], [b_bd], "pool")
            Vx(lambda e: e.memset(NBDI, 0.0), [], [b_bd], "pool")
            Vx(lambda e: e.memset(r0, 0.0), [], [b_bl[0]], "pool")
            Vx(lambda e: e.memset(i0, 0.0), [], [b_bl[0]], "pool")
            for ci_ in range(2):
                Vx(lambda e, ci_=ci_: e.memset(CHN[1][ci_], 0.0), [], [b_bl[1]], "pool")
            for ee in range(2):
                ps_ = slice(ee * 64, (ee + 1) * 64)
                cs_ = slice(ee * 16, (ee + 1) * 16)
                Vx(lambda e, ps_=ps_, cs_=cs_: e.tensor_copy(out=BDR[ps_, :, cs_], in_=BBR[ps_, :, :]), [b_ssm], [b_bd])
                Vx(lambda e, ps_=ps_, cs_=cs_: e.tensor_copy(out=r0[ps_, :, cs_], in_=BBR[ps_, :, :]), [b_ssm], [b_bl[0]])
                Vx(lambda e, ps_=ps_, cs_=cs_: e.tensor_copy(out=i0[ps_, :, cs_], in_=BBI[ps_, :, :]), [b_ssm], [b_bl[0]])
                Vx(lambda e, ps_=ps_, cs_=cs_: e.tensor_scalar(out=NBDI[ps_, :, cs_], in0=BBI[ps_, :, :], scalar1=-1.0,
                                                               scalar2=None, op0=ALU.mult), [b_ssm], [b_bd])
            l1r = L1R[:].unsqueeze(2).to_broadcast([128, 16, 32])
            l1i = L1I[:].unsqueeze(2).to_broadcast([128, 16, 32])
            cur = 0
            for s in range(7, -1, -1):
                if s > 0:
                    cmul(CHN[1 - cur], CHN[cur], l1r, l1i, b_bl[1 - cur], b_bl[cur])
                for ri in range(2):
                    src = CHN[cur][ri]
                    b_, bbx = next_bank()
                    for j in range(4):
                        E("pe", lambda e, b_=b_, src=src, j=j: e.matmul(
                            b_[:, j * 128:(j + 1) * 128],
                            lhsT=src[:, 4 * j:4 * j + 4, :].rearrange("p k c -> p (k c)"),
                            rhs=identf[:, :], start=True, stop=True),
                          reads=[b_bl[cur], b_identf], writes=[bbx])
                    E("act", lambda e, b_=b_, s=s, ri=ri: e.copy(
                        out=BLT[:, :, s, ri, :], in_=b_[:, :].rearrange("p (j c) -> p j c", j=4)),
                      reads=[bbx, b_blt], writes=[b_blt])
                cur = 1 - cur
            V(lambda e: e.tensor_copy(out=L8R[:], in_=L1R[:]))
            V(lambda e: e.tensor_copy(out=L8I[:], in_=L1I[:]))
            for _ in range(3):
                tt(T0[:], L8R[:], L8R[:], ALU.mult)
                tt(T1[:], L8I[:], L8I[:], ALU.mult)
                tt(T2[:], L8R[:], L8I[:], ALU.mult)
                tt(L8R[:], T0[:], T1[:], ALU.subtract)
                ts(L8I[:], T2[:], 2.0, ALU.mult)

        def prep_part2():
            act(RR[:], AA[:], AF.Exp, scale=8.0)
            ts(T0[:], TH[:], 8.0, ALU.mult)
            range_reduce(TH8[:], T0[:], 0.0, TY[:], TI[:], TF[:], TM[:])
            V(lambda e: e.iota(IOT[:], pattern=[[1, 64]], base=1, channel_multiplier=0,
                               allow_small_or_imprecise_dtypes=True), "pool")
            for hh in range(2):
                hs_ = slice(hh * 8, hh * 8 + 8)
                ang = L2F[:, 0:512].rearrange("p (a c) -> p a c", c=64)
                ay = L2F[:, 512:1024].rearrange("p (a c) -> p a c", c=64)
                am = L2F[:, 1024:1536].rearrange("p (a c) -> p a c", c=64)
                ai = TIB[:, :].rearrange("p (a c) -> p a c", c=64)
                x_ = [b_l2]
                tt(ang, TH8[:, hs_].unsqueeze(2).to_broadcast([128, 8, 64]),
                   IOT[:].unsqueeze(1).to_broadcast([128, 8, 64]), ALU.mult, extra=x_)
                for off_, dstt in ((0.0, SINT), (0.5 * PI, COST)):
                    ts(ay, ang, off_, ALU.add, extra=x_)
                    ts(dstt[:, hs_, :], ay, 1.0 / TWO_PI, ALU.mult, extra=x_)
                    V(lambda e, dstt=dstt, hs_=hs_, ai=ai: e.tensor_copy(out=ai, in_=dstt[:, hs_, :]), extra=x_)
                    V(lambda e, dstt=dstt, hs_=hs_, ai=ai: e.tensor_copy(out=dstt[:, hs_, :], in_=ai), extra=x_)
                    V(lambda e, dstt=dstt, hs_=hs_, ay=ay: e.scalar_tensor_tensor(
                        out=ay, in0=dstt[:, hs_, :], scalar=-TWO_PI, in1=ay, op0=ALU.mult, op1=ALU.add), extra=x_)
                    ts(am, ay, PI, ALU.is_gt, TWO_PI, ALU.mult, extra=x_)
                    tt(ay, ay, am, ALU.subtract, extra=x_)
                    ts(am, ay, -PI, ALU.is_lt, TWO_PI, ALU.mult, extra=x_)
                    tt(ay, ay, am, ALU.add, extra=x_)
                    act(dstt[:, hs_, :], ay, AF.Sin, extra=x_)
            for k in range(4):
                V(lambda e, k=k: e.tensor_copy(out=RTAB[:, 4 * k:4 * k + 4, :],
                                               in_=RR[:, k:16:4].unsqueeze(2).to_broadcast([128, 4, 64])))
            V(lambda e: e.memset(RTAB[:, :, 0:1], 0.0))

        def prep_part3():
            E("pool", lambda e: e.memset(KT[:], 0.0), writes=[b_kt])
            V(lambda e: e.memset(K0F, 0.0), "pool")
            for ci_ in range(2):
                for cj_ in range(2):
                    Vx(lambda e, ci_=ci_, cj_=cj_: e.memset(CHN[ci_][cj_], 0.0), [], [b_bl[ci_]], "pool")
            (r0, i0) = CHN[0]
            for ee in range(2):
                ps_ = slice(ee * 64, (ee + 1) * 64)
                cs_ = slice(ee * 16, (ee + 1) * 16)
                Vx(lambda e, ps_=ps_, cs_=cs_: e.tensor_copy(out=r0[ps_, :, cs_], in_=CRT[ps_, :, :]), [b_ssm], [b_bl[0]])
                Vx(lambda e, ps_=ps_, cs_=cs_: e.tensor_copy(out=i0[ps_, :, cs_], in_=CIT[ps_, :, :]), [b_ssm], [b_bl[0]])
            l1r = L1R[:].unsqueeze(2).to_broadcast([128, 16, 32])
            l1i = L1I[:].unsqueeze(2).to_broadcast([128, 16, 32])
            cur = 0
            for tau in range(9):
                c_re, c_im = CHN[cur]
                if tau < 8:
                    cmul(CHN[1 - cur], CHN[cur], l1r, l1i, b_bl[1 - cur], b_bl[cur])
                if tau >= 1:
                    s = tau - 1
                    E("act", lambda e, s=s, c_re=c_re: e.copy(out=CH[:, :, s, 0, :], in_=c_re), reads=[b_bl[cur]], writes=[b_ch])
                    E("act", lambda e, s=s, c_im=c_im: e.mul(out=CH[:, :, s, 1, :], in_=c_im, mul=-1.0), reads=[b_bl[cur]], writes=[b_ch])
                if tau <= 7:
                    bk, bb = next_bank()
                    for pr in range(16):
                        j, k = divmod(pr, 4)
                        o = bk[32 * k:32 * k + 32, j * 32:(j + 1) * 32]
                        E("pe", lambda e, o=o, pr=pr, k=k, c_re=c_re: e.matmul(
                            o, lhsT=BDR[:, pr, :], rhs=c_re[:, pr, :], start=True, stop=False,
                            tile_position=(0, 32 * k), skip_group_check=True),
                          reads=[b_bd, b_bl[cur]], writes=[bb], signal=False)
                        E("pe", lambda e, o=o, pr=pr, k=k, c_im=c_im: e.matmul(
                            o, lhsT=NBDI[:, pr, :], rhs=c_im[:, pr, :], start=False, stop=True,
                            tile_position=(0, 32 * k), skip_group_check=True),
                          reads=[b_bd, b_bl[cur]], writes=[bb], signal=(pr == 15))
                    for k in range(4):
                        pq = slice(32 * k, 32 * k + 32)
                        src_v = bk[pq, 0:128].rearrange("p (j c) -> p j c", j=4)
                        if tau == 0:
                            E("dve", lambda e, pq=pq, src_v=src_v: e.tensor_copy(out=K0F[pq, :, pq], in_=src_v),
                              reads=[bb, b_ssm], writes=[b_ssm])
                        else:
                            E("act", lambda e, pq=pq, src_v=src_v, tau=tau: e.copy(out=KT[pq, :, tau, pq], in_=src_v),
                              reads=[bb, b_kt], writes=[b_kt])
                cur = 1 - cur
            for j in range(4):
                E("dve", lambda e, j=j: e.scalar_tensor_tensor(
                    out=KT[:, j, 0, :], in0=identf[:, :], scalar=DV[:, j:j + 1], in1=K0F[:, j, :],
                    op0=ALU.mult, op1=ALU.add), reads=[b_identf, b_parm, b_ssm, b_kt], writes=[b_kt])

        for ri, ct in enumerate((CRT, CIT)):
            for j in range(4):
                bk, bb = next_bank()
                E("pe", lambda e, bk=bk, ri=ri, j=j: e.matmul(bk[:, 0:64], lhsT=CST[:, ri, j, :], rhs=identf[0:64, 0:64],
                                                              start=True, stop=True),
                  reads=[b_cst, b_identf], writes=[bb])
                E("dve", lambda e, bk=bk, ct=ct, j=j: e.tensor_copy(
                    out=ct[:, 4 * j:4 * j + 4, :], in_=bk[:, 0:64].rearrange("p (k h) -> p k h", k=4)),
                  reads=[bb, b_ssm], writes=[b_ssm])
        H0 = sb("H0", [128, 2, 16, 16]); H0B = sb("H0B", [128, 2, 16, 16], BF16); b_h0 = Buf("H0")
        CCH = sb("CCH", [128, 4, 2, 16]); b_cch = Buf("CCH")
        H0N = XB[0:16, 1:3, :].rearrange("p t d -> p (t d)")
        h0_ch = P.chan()

        def load_h0():
            for ri in range(2):
                src_h = h0r if ri == 0 else h0i
                E("sp", lambda e, src_h=src_h: e.dma_start(out=H0N[:, :], in_=src_h), writes=[b_xb[1], b_xb[2]], chan=h0_ch)
                bk, bb = next_bank()
                for pr in range(16):
                    E("pe", lambda e, bk=bk, pr=pr: e.matmul(
                        bk[:, pr * 16:(pr + 1) * 16], lhsT=H0N[:, pr * 128:(pr + 1) * 128],
                        rhs=identf[0:16, 0:16], start=True, stop=True),
                      reads=[b_xb[1], b_xb[2], b_identf], writes=[bb])
                E("dve", lambda e, bk=bk, ri=ri: e.tensor_copy(
                    out=H0[:, ri, :, :], in_=bk[:, 0:256].rearrange("p (a b) -> p a b", a=16)),
                  reads=[bb], writes=[b_h0])
                E("act", lambda e, bk=bk, ri=ri: e.copy(
                    out=H0B[:, ri, :, :], in_=bk[:, 0:256].rearrange("p (a b) -> p a b", a=16)),
                  reads=[bb], writes=[b_h0])

        bk, bb = next_bank()
        for kk in range(2):
            for j in range(4):
                E("pe", lambda e, bk=bk, kk=kk, j=j: e.matmul(
                    bk[:, (j * 2 + kk) * 16:(j * 2 + kk + 1) * 16],
                    lhsT=CCN[:, kk * 512 + j * 128:kk * 512 + (j + 1) * 128],
                    rhs=identf[0:16, 0:16], start=True, stop=True),
                  reads=[b_osc, b_identf], writes=[bb])
        E("dve", lambda e, bk=bk: e.tensor_copy(
            out=CCH[:], in_=bk[:, 0:128].rearrange("p (j k b) -> p j k b", j=4, k=2)),
          reads=[bb], writes=[b_cch])


        XT = ARENA[:, 0:2048].bitcast(BF16).rearrange("p (k c) -> p k c", c=512); b_xt = Buf("XT")
        SS = sb("SS", [128, 12]); b_ss3 = [Buf("SS0"), Buf("SS1"), Buf("SS2")]
        XC = sb("XC", [128, 512]); b_xc = Buf("xc")
        VX = sb("VX", [128, 4, 528]); b_vx = [Buf("vx%d" % j) for j in range(4)]
        CA = sb("CA", [128, 512]); b_ca = Buf("ca")
        MIX = ARENA[:, 2048:4096].bitcast(BF16).rearrange("p (k c) -> p k c", c=512); b_mix = [Buf("mix%d" % j) for j in range(8)]
        UB = sb("UB", [128, 4, 512], BF16); b_ub = [Buf("ub%d" % j) for j in range(4)]
        WIN = L2T[:, 0, :, :]; WW = L2T[:, 1, :, :]; TT_ = L2T[:, 2, :, :]
        ZB = sb("ZB", [128, 2, 16, 65], BF16); b_zb = Buf("ZB")
        ZP = sb("ZP", [128, 2, 16]); b_zp = Buf("ZP")
        ZI = sb("ZI", [128, 2, 4])
        ZN = sb("ZN", [128, 2, 16, 16]); b_zn = Buf("ZN")
        XNS = [sb("XN0", [128, D], BF16), ZN[:, :, :, :].rearrange("p a b c -> p (a b c)").bitcast(BF16)]
        b_xns = [Buf("xn0"), Buf("xn1")]
        ZG = ARENA[:, 4096:5120].bitcast(BF16).rearrange("p (k c) -> p k c", c=512); b_zg = [Buf("zg%d" % j) for j in range(4)]
        HT = SETUPT[:, 0:2048].bitcast(BF16).rearrange("p (k c) -> p k c", c=512); b_ht = Buf("HT")
        YO = [SETUPT[:, 2048:3072], SETUPT[:, 3072:4096]]; b_yo = [Buf("yo0"), Buf("yo1")]
        SIL = [SETUPT[:, 4096:4608], SETUPT[:, 4608:5120]]; b_sil = [Buf("sil0"), Buf("sil1")]
        GT = [SETUPT[:, 5120:5376].bitcast(BF16), SETUPT[:, 5376:5632].bitcast(BF16)]; b_gt = [Buf("gt0"), Buf("gt1")]
        ACT_ = ARENA[:, :].bitcast(BF16)
        b_actf = [Buf("act%d" % f) for f in range(NF)]
        OST = XB[0:16, 1:3, :].rearrange("p t d -> p (t d)")

        E("dve", lambda e: e.memset(VX[:], 0.0), writes=b_vx)
        E("dve", lambda e: e.memset(ZP[:], 0.0), writes=[b_zp])
        E("dve", lambda e: e.memset(ZB[:], 0.0), writes=[b_zb])

        def lookahead(stream, first, reqs, depth):
            q = [stream.get(first, *reqs[i]) for i in range(min(depth, len(reqs)))]
            for i in range(len(reqs)):
                if i + depth < len(reqs):
                    q.append(stream.get(first, *reqs[i + depth]))
                yield q.pop(0)

        r_win = WStream("RWIN", 2, [128, 8, 128])
        r_gu = WStream("RGU", 2, [128, 2, 8, 128])
        r_w = WStream("RW", 4, [128, 512])
        XTO = sb("XTO", [128, 8, 512], BF16); b_xto = Buf("XTO")
        mixer_bufs = [b_xt] + b_mix + b_zg
        bgc_ch = P.chan(final=True)
        b_wos = [Buf("scr_wo%d" % k) for k in range(8)]
        b_wds = [Buf("scr_wd%d" % f) for f in range(NF)]
        for k in range(8):
            E("pool", lambda e, k=k: e.dma_start(out=wout_s[k, :, :], in_=wout_d[k * 128:(k + 1) * 128, :]),
              writes=[b_wos[k]], chan=bgc_ch)
            for half in range(2):
                scr_bufs[("RW", ("wo", k, half))] = b_wos[k]
        for f in range(NF):
            E("pool", lambda e, f=f: e.dma_start(out=wd_s[f, :, :], in_=wd_d[f * 128:(f + 1) * 128, :]),
              writes=[b_wds[f]], chan=bgc_ch)
            for half in range(2):
                scr_bufs[("RW", ("wd", f, half))] = b_wds[f]
        XB2 = [STG[0][:, :], STG[1][:, :], STG[2][:, :], SETUPT[:, 5632:6656]]
        b_xb2 = [Buf("xb2_%d" % t) for t in range(4)]
        xsel = [(lambda t: XB[:, t, :], b_xb), (lambda t: XB2[t], b_xb2)]
        out_ch = [P.chan(), P.chan()]
        osc_ch = P.chan()
        ost_ch = P.chan()

        def rmsnorm_stats_all(nt, colbase, xt, bx):
            cols = SS[:, colbase:colbase + nt]
            E("dve", lambda e: e.memset(SS[:, colbase:colbase + 4], 0.0), reads=[b_ss3[colbase // 4]], writes=[b_ss3[colbase // 4]])
            for t in range(nt):
                E("act", lambda e, t=t: e.activation(out=XNS[t % 2][:, :], in_=xt(t), func=AF.Square,
                                                     accum_out=SS[:, colbase + t:colbase + t + 1]),
                  reads=[bx[t], b_ss3[colbase // 4]], writes=[b_xns[t % 2], b_ss3[colbase // 4]])
            E("dve", lambda e: e.tensor_scalar(out=cols, in0=cols, scalar1=1.0 / D, scalar2=EPS, op0=ALU.mult,
                                               op1=ALU.add), reads=[b_ss3[colbase // 4]], writes=[b_ss3[colbase // 4]])
            E("act", lambda e: e.activation(out=cols, in_=cols, func=AF.Sqrt), reads=[b_ss3[colbase // 4]], writes=[b_ss3[colbase // 4]])
            E("dve", lambda e: e.reciprocal(out=cols, in_=cols), reads=[b_ss3[colbase // 4]], writes=[b_ss3[colbase // 4]])

        def norm_transpose(nt, DST, b_dst, colbase, xt, bx, do_stats=True):
            if do_stats:
                rmsnorm_stats_all(nt, colbase, xt, bx)
            def scale(t):
                E("dve", lambda e, t=t: e.tensor_scalar(
                    out=XNS[t % 2][:, :], in0=xt(t), scalar1=SS[:, colbase + t:colbase + t + 1], scalar2=None,
                    op0=ALU.mult), reads=[bx[t], b_ss3[colbase // 4]], writes=[b_xns[t % 2]])

            scale(0)
            for t in range(nt):
                if t + 1 < nt:
                    scale(t + 1)
                for hh in range(2):
                    bk, bb = next_bank()
                    for kk in range(4):
                        k = hh * 4 + kk
                        E("pe", lambda e, bk=bk, kk=kk, k=k, t=t: e.matmul(
                            bk[:, kk * 128:(kk + 1) * 128], lhsT=XNS[t % 2][:, k * 128:(k + 1) * 128], rhs=identb[:, :],
                            start=True, stop=True), reads=[b_xns[t % 2], b_identb], writes=[bb], signal=(kk == 3))
                    if hh == 0:
                        E("act", lambda e, bk=bk, hh=hh, t=t: e.copy(
                            out=DST[:, hh * 4:hh * 4 + 4, t * 128:(t + 1) * 128],
                            in_=bk[:, :].rearrange("p (k c) -> p k c", k=4)), reads=[bb], writes=[b_dst])
                    else:
                        E("dve", lambda e, bk=bk, hh=hh, t=t: e.tensor_copy(
                            out=DST[:, hh * 4:hh * 4 + 4, t * 128:(t + 1) * 128],
                            in_=bk[:, :].rearrange("p (k c) -> p k c", k=4)), reads=[bb], writes=[b_dst])

        blocks = [("s", 0)] + [("p", i) for i in range(4)]
        import os as _os
        KSTOP = int(_os.environ.get("KSTOP", "999"))
        bidx = {("p", 0): 0, ("p", 1): 1, ("p", 2): 2, ("p", 3): 3, ("s", 0): 4}

        def do_block(kind, bi):
            first = (kind == "s")
            nt = 4 if kind == "p" else 1
            NT = 128 * nt
            NCH = NT // 8
            xt, bx = xsel[bi % 2] if kind == "p" else xsel[0]
            XTv, bxt = XTO, b_xto
            if kind == "p":
                if bi == 1:
                    handover(b_stgs + [b_tmp], b_xb2)
                src = xp[bi * 512:(bi + 1) * 512, :].rearrange("(t p) d -> p t d", p=128)
                for t in range(4):
                    E("sp", lambda e, src=src, t=t: e.dma_start(out=xt(t), in_=src[:, t, :]), writes=[bx[t]],
                      chan=xin_ch[(bi % 2) * 4 + t])
            rmsnorm_stats_all(nt, 0, xt, bx)
            yield "stats"

            if kind == "p" and bi == 0:
                handover([b_zn], [b_xns[1]])
                for j in range(4):
                    E("dve", lambda e, j=j: e.memset(VX[:, j, 0:2], 0.0), writes=[b_vx[j]])
            norm_transpose(nt, XTv, bxt, 0, xt, bx, do_stats=False)

            def vview(j, lo):
                if kind == "p":
                    return VX[:, j, lo:lo + 512]
                return VX[:, j, 0:160].rearrange("p (b l) -> p b l", l=10)[:, :, lo:lo + 8]

            def tokview(ap2d):
                if kind == "p":
                    return ap2d
                return ap2d.rearrange("p (b l) -> p b l", l=8)

            m_order = [12, 13, 14, 15] + [m for j in range(4) for m in (j, 4 + j, 8 + j)]
            win_it = lookahead(r_win, first, [(m, (lambda t, m=m: [(
                t[:, :, :], w_in_d[:, m * 128:(m + 1) * 128].rearrange("(k p) c -> p k c", p=128), G1C[:, :])]),
                win_s[m, :, :, :]) for m in m_order], 1)
            m_seen = []

            def proj_tile(m):
                assert m == m_order[len(m_seen)]
                m_seen.append(m)
                wt, wb = next(win_it)
                bk, bb = next_bank()
                for k in range(8):
                    E("pe", lambda e, bk=bk, wt=wt, k=k: e.matmul(bk[:, 0:NT], lhsT=wt[:, k, :], rhs=XTv[:, k, 0:NT],
                                                                  start=(k == 0), stop=(k == 7)),
                      reads=[wb[k], bxt], writes=[bb], signal=(k == 7))
                return bk, bb

            if kind == "s":
                for j in range(4):
                    E("dve", lambda e, j=j: e.tensor_copy(
                        out=VX[:, j, 0:160].rearrange("p (b l) -> p b l", l=10)[:, :, 0:2],
                        in_=CCH[:, j, :, :].rearrange("p k b -> p b k")), reads=[b_cch], writes=[b_vx[j]])
            for j in range(4):
                bU, bbU = proj_tile(12 + j)
                if j % 2 == 0:
                    E("act", lambda e, bU=bU, j=j: e.copy(out=UB[:, j, 0:NT].rearrange("p (s c) -> p c s", s=8),
                                                          in_=bU[:, 0:NT].rearrange("p (c s) -> p c s", s=8)),
                      reads=[bbU], writes=[b_ub[j]])
                else:
                    E("dve", lambda e, bU=bU, j=j: e.tensor_copy(out=UB[:, j, 0:NT].rearrange("p (s c) -> p c s", s=8),
                                                                 in_=bU[:, 0:NT].rearrange("p (c s) -> p c s", s=8)),
                      reads=[bbU], writes=[b_ub[j]])

            yield "uproj"
            if kind == "s":
                load_h0()
            ebank = [next_bank(pin=True) for _ in range(4)]
            for ri in range(2):
                for j in range(4):
                    for s in range(8):
                        for k in range(4):
                            bk, bb = ebank[k]
                            lastm = (ri == 1 and j == 3 and s == 7)
                            E("pe", lambda e, bk=bk, j=j, k=k, s=s, ri=ri: e.matmul(
                                bk[:, (ri * 4 + j) * 64:(ri * 4 + j) * 64 + NCH], lhsT=BLT[32 * k:32 * k + 32, j, s, ri, :],
                                rhs=UB[32 * k:32 * k + 32, j, s * NCH:(s + 1) * NCH], start=(s == 0), stop=(s == 7),
                                tile_position=(32 * k, 0), skip_group_check=True),
                              reads=[b_blt, b_ub[j]], writes=[bb], signal=lastm)

            if kind == "p":
                for ri in range(2):
                    E("act", lambda e, ri=ri: e.copy(out=ZB[:, ri, :, 0:1], in_=ZP[:, ri, :].unsqueeze(2)),
                      reads=[b_zp], writes=[b_zb])
                def l2_group(k):
                    bke, bbe = ebank[k]
                    bre = bke[:, 0:256]
                    bim = bke[:, 256:512]
                    ks = slice(k, 16, 4)
                    v3 = lambda t: t.rearrange("p (a c) -> p a c", c=64)
                    cosv = COST[:, ks, :]
                    sinv = SINT[:, ks, :]
                    L2 = lambda fn, rd=(): E("dve", fn, reads=[b_l2, b_ssm, b_zp] + list(rd), writes=[b_l2])
                    mulop = lambda o, a_, b_, rd=(): L2(lambda e: e.tensor_tensor(out=o, in0=a_, in1=b_, op=ALU.mult), rd)
                    W0, W1, S0, S1 = v3(WIN[:, 0, :]), v3(WIN[:, 1, :]), v3(TT_[:, 0, :]), v3(TT_[:, 1, :])
                    mulop(W0, v3(bre), cosv, [bbe])
                    mulop(S0, v3(bim), sinv, [bbe])
                    mulop(W1, v3(bim), cosv, [bbe])
                    L2(lambda e: e.tensor_tensor(out=WIN[:, 0, :], in0=WIN[:, 0, :], in1=TT_[:, 0, :], op=ALU.add))
                    mulop(S1, v3(bre), sinv, [bbe])
                    L2(lambda e, ks=ks: e.tensor_tensor(out=ZI[:, 0, :], in0=ZP[:, 0, ks], in1=RR[:, ks], op=ALU.mult))
                    L2(lambda e: e.tensor_tensor(out=WIN[:, 1, :], in0=WIN[:, 1, :], in1=TT_[:, 1, :], op=ALU.subtract))
                    L2(lambda e, ks=ks: e.tensor_tensor(out=ZI[:, 1, :], in0=ZP[:, 1, ks], in1=RR[:, ks], op=ALU.mult))
                    for ri in range(2):
                        L2(lambda e, ri=ri: e.tensor_tensor(
                            out=v3(WIN[:, ri, :])[:, :, 0:1], in0=v3(WIN[:, ri, :])[:, :, 0:1],
                            in1=ZI[:, ri, :].unsqueeze(2), op=ALU.add))
                    for ri in range(2):
                        L2(lambda e, ri=ri, k=k: e.tensor_tensor_scan(
                            out=WW[:, ri, :], data0=RTAB[:, 4 * k:4 * k + 4, :].rearrange("p a c -> p (a c)"),
                            data1=WIN[:, ri, :], initial=0.0, op0=ALU.mult, op1=ALU.add))
                    mulop(W0, v3(WW[:, 0, :]), cosv)
                    mulop(S0, v3(WW[:, 1, :]), sinv)
                    mulop(W1, v3(WW[:, 0, :]), sinv)
                    mulop(S1, v3(WW[:, 1, :]), cosv)
                    L2(lambda e: e.tensor_tensor(out=WIN[:, 0, :], in0=WIN[:, 0, :], in1=TT_[:, 0, :], op=ALU.subtract))
                    L2(lambda e: e.tensor_tensor(out=WIN[:, 1, :], in0=WIN[:, 1, :], in1=TT_[:, 1, :], op=ALU.add))
                    for ri in range(2):
                        E("act", lambda e, ri=ri, ks=ks: e.copy(out=ZB[:, ri, ks, 1:65], in_=v3(WIN[:, ri, :])),
                          reads=[b_l2], writes=[b_zb])
                        E("dve", lambda e, ri=ri, ks=ks: e.tensor_copy(out=ZP[:, ri, ks], in_=v3(WIN[:, ri, :])[:, :, 63]),
                          reads=[b_l2], writes=[b_zp])
                    unpin(bbe)
            yield "headpe"
            if kind == "p":
                for i_ in range(4):
                    l2_group(i_)
                    yield "l2_%d" % i_
            yield "headdone"
            handover(b_actf, mixer_bufs)
            def conv_group(j):
                bB, bbB = proj_tile(j)
                bC, bbC = proj_tile(4 + j)
                bX, bbX = proj_tile(8 + j)
                E("act", lambda e, bX=bX: e.copy(out=XC[:, 0:NT], in_=bX[:, 0:NT]), reads=[bbX], writes=[b_xc])
                E("dve", lambda e, bC=bC, j=j: e.tensor_tensor(
                    out=vview(j, 2), in0=tokview(bC[:, 0:NT]), in1=tokview(XC[:, 0:NT]), op=ALU.mult),
                  reads=[bbC, b_xc], writes=[b_vx[j]])
                E("dve", lambda e, j=j: e.tensor_scalar(
                    out=tokview(CA[:, 0:NT]), in0=vview(j, 2), scalar1=CW[:, j, 2:3], scalar2=None, op0=ALU.mult),
                  reads=[b_vx[j], b_parm], writes=[b_ca])
                E("dve", lambda e, j=j: e.scalar_tensor_tensor(
                    out=tokview(CA[:, 0:NT]), in0=vview(j, 1), scalar=CW[:, j, 1:2], in1=tokview(CA[:, 0:NT]),
                    op0=ALU.mult, op1=ALU.add), reads=[b_vx[j], b_parm, b_ca], writes=[b_ca])
                E("dve", lambda e, j=j: e.scalar_tensor_tensor(
                    out=tokview(CA[:, 0:NT]), in0=vview(j, 0), scalar=CW[:, j, 0:1], in1=tokview(CA[:, 0:NT]),
                    op0=ALU.mult, op1=ALU.add), reads=[b_vx[j], b_parm, b_ca], writes=[b_ca])
                E("dve", lambda e, bB=bB, j=j: e.tensor_tensor(
                    out=MIX[:, j, 0:NT], in0=bB[:, 0:NT], in1=CA[:, 0:NT], op=ALU.mult),
                  reads=[bbB, b_ca], writes=[b_mix[j]])
            if first:
                prep_part3()
            if kind == "p":
                for i_ in range(4):
                    conv_group(i_)
                zsrc = lambda ri, pr: ZB[:, ri, pr, 0:64]
                b_zsrc = b_zb
            else:
                zsrc = lambda ri, pr: H0B[:, ri, pr, :]
                b_zsrc = b_h0
                for k in range(4):
                    bke, bbe = ebank[k]
                    ks = slice(k, 16, 4)
                    ev = lambda t: t.rearrange("p (a c) -> p a c", c=64)[:, :, 0:16]
                    bre = ev(bke[:, 0:256])
                    bim = ev(bke[:, 256:512])
                    lr = L8R[:, ks].unsqueeze(2).to_broadcast([128, 4, 16])
                    li = L8I[:, ks].unsqueeze(2).to_broadcast([128, 4, 16])
                    t0 = WIN[:, 0, 0:64].rearrange("p (a c) -> p a c", c=16)
                    t1 = WIN[:, 1, 0:64].rearrange("p (a c) -> p a c", c=16)
                    S2 = lambda fn, rd=(): E("dve", fn, reads=[b_l2, b_ssm, b_h0, b_zn] + list(rd), writes=[b_l2, b_zn])
                    S2(lambda e, ks=ks, lr=lr: e.tensor_tensor(out=t0, in0=H0[:, 0, ks, :], in1=lr, op=ALU.mult))
                    S2(lambda e, ks=ks, li=li: e.tensor_tensor(out=t1, in0=H0[:, 1, ks, :], in1=li, op=ALU.mult))
                    S2(lambda e: e.tensor_tensor(out=t0, in0=t0, in1=t1, op=ALU.subtract))
                    S2(lambda e, ks=ks, bre=bre: e.tensor_tensor(out=ZN[:, 0, ks, :], in0=t0, in1=bre, op=ALU.add), [bbe])
                    S2(lambda e, ks=ks, li=li: e.tensor_tensor(out=t0, in0=H0[:, 0, ks, :], in1=li, op=ALU.mult))
                    S2(lambda e, ks=ks, lr=lr: e.tensor_tensor(out=t1, in0=H0[:, 1, ks, :], in1=lr, op=ALU.mult))
                    S2(lambda e: e.tensor_tensor(out=t0, in0=t0, in1=t1, op=ALU.add))
                    S2(lambda e, ks=ks, bim=bim: e.tensor_tensor(out=ZN[:, 1, ks, :], in0=t0, in1=bim, op=ALU.add), [bbe])
                    unpin(bbe)
                for i_ in range(4):
                    conv_group(i_)

            if kind == "p" and bi == 3:
                bk, bb = next_bank()
                for j in range(4):
                    E("pe", lambda e, bk=bk, j=j: e.matmul(bk[0:2, j * 128:(j + 1) * 128], lhsT=VX[:, j, 512:514],
                                                           rhs=identf[:, :], start=True, stop=True),
                      reads=[b_vx[j], b_identf], writes=[bb])
                E("dve", lambda e, bk=bk: e.tensor_copy(out=OSC[0:2, 0:512], in_=bk[0:2, 0:512]), reads=[bb, b_osc], writes=[b_osc])
                E("pool", lambda e: e.dma_start(out=convp_o, in_=OSC[0:2, 0:512]), reads=[b_osc], chan=osc_ch)
            if kind == "p":
                for j in range(4):
                    E("dve", lambda e, j=j: e.tensor_copy(out=VX[:, j, 0:2], in_=VX[:, j, 512:514]),
                      reads=[b_vx[j]], writes=[b_vx[j]])
            if kind == "s":
                for kk in range(2):
                    bk, bb = next_bank()
                    for j in range(4):
                        E("pe", lambda e, bk=bk, j=j, kk=kk: e.matmul(
                            bk[0:16, j * 128:(j + 1) * 128],
                            lhsT=VX[:, j, 0:160].rearrange("p (b l) -> p b l", l=10)[:, :, 8 + kk],
                            rhs=identf[:, :], start=True, stop=True),
                          reads=[b_vx[j], b_identf], writes=[bb])
                    E("dve", lambda e, bk=bk, kk=kk: e.tensor_copy(out=OSC[:, kk * 512:(kk + 1) * 512], in_=bk[0:16, 0:512]),
                      reads=[bb, b_osc], writes=[b_osc])
                E("pool", lambda e: e.dma_start(out=convs_o, in_=OSC[:, :]), reads=[b_osc], chan=osc_ch)

            yield "convdone"
            ybank = []
            for j in range(4):
                bk, bb = next_bank()
                ybank.append((bk, bb))
                E("pe", lambda e, bk=bk, j=j: e.matmul(
                    bk[:, 0:NT], lhsT=KT[:, j, 0, :], rhs=UB[:, j, 0:NT], start=True, stop=False, skip_group_check=True),
                  reads=[b_kt, b_ub[j]], writes=[bb], signal=False)
                for tau in range(1, 8):
                    E("pe", lambda e, bk=bk, j=j, tau=tau: e.matmul(
                        bk[:, tau * NCH:NT], lhsT=KT[:, j, tau, :], rhs=UB[:, j, 0:(8 - tau) * NCH], start=False, stop=False,
                        skip_group_check=True), reads=[b_kt, b_ub[j]], writes=[bb], signal=False)

            for j in range(4):
                bk, bb = ybank[j]
                for s in range(8):
                    for ri in range(2):
                        for k in range(4):
                            pr = 4 * j + k
                            last = (k == 3 and s == 7 and ri == 1)
                            E("pe", lambda e, bk=bk, k=k, pr=pr, s=s, ri=ri, last=last: e.matmul(
                                bk[32 * k:32 * k + 32, s * NCH:(s + 1) * NCH], lhsT=CH[:, pr, s, ri, :], rhs=zsrc(ri, pr)[:, 0:NCH],
                                start=False, stop=last, tile_position=(0, 32 * k), skip_group_check=True),
                              reads=[b_ch, b_zsrc], writes=[bb], signal=last)
                E("act", lambda e, bk=bk, j=j: e.activation(
                    out=ZG[:, j, 0:NT].rearrange("p (c s) -> p c s", s=8),
                    in_=bk[:, 0:NT].rearrange("p (s c) -> p c s", s=8), func=AF.Gelu_apprx_tanh),
                  reads=[bb], writes=[b_zg[j]])

            if kind == "p" and bi == 3:
                bk, bb = next_bank()
                for ri in range(2):
                    E("pe", lambda e, bk=bk, ri=ri: e.matmul(bk[0:16, ri * 128:(ri + 1) * 128], lhsT=ZP[:, ri, :],
                                                             rhs=identf[:, :], start=True, stop=True),
                      reads=[b_zp, b_identf], writes=[bb])
                E("dve", lambda e, bk=bk: e.tensor_copy(out=OSC[:, 512:768], in_=bk[0:16, 0:256]), reads=[bb, b_osc], writes=[b_osc])
                E("pool", lambda e: e.dma_start(out=rep_o, in_=OSC[:, 512:640]), reads=[b_osc], chan=osc_ch)
                E("pool", lambda e: e.dma_start(out=imp_o, in_=OSC[:, 640:768]), reads=[b_osc], chan=osc_ch)
            if kind == "s":
                for ri in range(2):
                    for q in range(4):
                        bk, bb = next_bank()
                        for pp in range(4):
                            pr = q * 4 + pp
                            E("pe", lambda e, bk=bk, ri=ri, pr=pr, pp=pp: e.matmul(
                                bk[0:16, pp * 128:(pp + 1) * 128], lhsT=ZN[:, ri, pr, :], rhs=identf[:, :],
                                start=True, stop=True), reads=[b_zn, b_identf], writes=[bb])
                        E("dve", lambda e, bk=bk, q=q: e.tensor_copy(out=OST[:, q * 512:(q + 1) * 512], in_=bk[0:16, :]),
                          reads=[bb, b_xb[1], b_xb[2]], writes=[b_xb[1], b_xb[2]])
                    dst_o = res_o if ri == 0 else ims_o
                    E("pool", lambda e, dst_o=dst_o: e.dma_start(out=dst_o, in_=OST[:, :]), reads=[b_xb[1], b_xb[2]], chan=ost_ch)

            for m in range(4):
                bk, bb = next_bank()
                for k in range(4):
                    E("pe", lambda e, bk=bk, m=m, k=k: e.matmul(bk[:, 0:NT], lhsT=WGLU[:, k, m * 128:(m + 1) * 128],
                                                                rhs=ZG[:, k, 0:NT], start=(k == 0), stop=(k == 3)),
                      reads=[b_wglu] + b_zg, writes=[bb], signal=(k == 3))
                gi = m % 2
                E("act", lambda e, bk=bk, m=m, gi=gi: e.activation(out=GT[gi][:, 0:NT], in_=bk[:, 0:NT], func=AF.Sigmoid,
                                                                   bias=BG[:, m:m + 1]), reads=[bb, b_parm], writes=[b_gt[gi]])
                E("dve", lambda e, m=m, gi=gi: e.tensor_tensor(out=MIX[:, 4 + m, 0:NT], in0=ZG[:, m, 0:NT], in1=GT[gi][:, 0:NT],
                                                               op=ALU.mult), reads=[b_zg[m], b_gt[gi]], writes=[b_mix[4 + m]])

            wo_it = lookahead(r_w, False, [(("wo", k, half), (lambda t, k=k, half=half: [(
                t[:, :], wout_d[k * 128:(k + 1) * 128, half * 512:(half + 1) * 512], None)]),
                wout_s[k, :, half * 512:(half + 1) * 512]) for half in range(2) for k in range(8)], 3)
            for half in range(2):
                accs = [next_bank() for _ in range(nt)]
                for k in range(8):
                    wt, wb = next(wo_it)
                    for t in range(nt):
                        bk, bb = accs[t]
                        E("pe", lambda e, bk=bk, wt=wt, k=k, t=t: e.matmul(
                            bk[:, :], lhsT=MIX[:, k, t * 128:(t + 1) * 128], rhs=wt[:, :], start=(k == 0), stop=(k == 7)),
                          reads=[wb[0], b_mix[k]], writes=[bb], signal=(k == 7 or t == nt - 1))
                for t in range(nt):
                    bk, bb = accs[t]
                    E("dve", lambda e, bk=bk, t=t, half=half: e.tensor_tensor(
                        out=xt(t)[:, half * 512:(half + 1) * 512], in0=bk[:, :], in1=xt(t)[:, half * 512:(half + 1) * 512],
                        op=ALU.add), reads=[bb, bx[t]], writes=[bx[t]])

            yield "wodone"
            if first:
                handover([b_ssm, b_tmp, b_bd, b_cst] + b_bl, [b_ht] + b_yo + b_sil + b_gt)
            handover(mixer_bufs, b_actf)
            norm_transpose(nt, HT, b_ht, 4, xt, bx)
            yield "n2done"
            gu_it = lookahead(r_gu, first, [(f, (lambda t, f=f: [
                (t[:, 0, :, :], wg_d[:, f * 128:(f + 1) * 128].rearrange("(k p) c -> p k c", p=128), G2C[:, :]),
                (t[:, 1, :, :], wu_d[:, f * 128:(f + 1) * 128].rearrange("(k p) c -> p k c", p=128), G2C[:, :])]),
                wgu_s[f, :, :, :, :]) for f in range(NF)], 1)
            if first:
                deferred[0] = []
                prep_part2()
                p2_thunks = deferred[0]
                deferred[0] = None
            for f in range(NF):
                if first:
                    for _ in range(4):
                        if p2_thunks:
                            p2_thunks.pop(0)()
                if f in (3, 8, 13, 18):
                    yield "gu_f%d" % f
                wt, wb = next(gu_it)
                bg, bbg = next_bank()
                bu, bbu = next_bank()
                for gi, (bk, bb) in enumerate(((bg, bbg), (bu, bbu))):
                    for k in range(8):
                        E("pe", lambda e, bk=bk, wt=wt, gi=gi, k=k: e.matmul(
                            bk[:, 0:NT], lhsT=wt[:, gi, k, :], rhs=HT[:, k, 0:NT], start=(k == 0), stop=(k == 7)),
                          reads=[wb[gi * 8 + k], b_ht], writes=[bb], signal=(k == 7))
                si = f % 2
                E("act", lambda e, bg=bg, si=si: e.activation(out=SIL[si][:, 0:NT], in_=bg[:, 0:NT], func=AF.Silu),
                  reads=[bbg], writes=[b_sil[si]])
                E("dve", lambda e, bu=bu, si=si, f=f: e.tensor_tensor(
                    out=ACT_[:, f * 512:f * 512 + NT], in0=bu[:, 0:NT], in1=SIL[si][:, 0:NT], op=ALU.mult),
                  reads=[bbu, b_sil[si]], writes=[b_actf[f]])
            if first:
                while p2_thunks:
                    p2_thunks.pop(0)()
            yield "gudone"
            wd_it = lookahead(r_w, False, [(("wd", f, half), (lambda t, f=f, half=half: [(
                t[:, :], wd_d[f * 128:(f + 1) * 128, half * 512:(half + 1) * 512], None)]),
                wd_s[f, :, half * 512:(half + 1) * 512]) for half in range(2) for f in range(NF)], 3)
            for half in range(2):
                accs = [next_bank() for _ in range(nt)]
                for f in range(NF):
                    wt, wb = next(wd_it)
                    for t in range(nt):
                        bk, bb = accs[t]
                        E("pe", lambda e, bk=bk, wt=wt, f=f, t=t: e.matmul(
                            bk[:, :], lhsT=ACT_[:, f * 512 + t * 128:f * 512 + (t + 1) * 128], rhs=wt[:, :],
                            start=(f == 0), stop=(f == NF - 1)),
                          reads=[wb[0], b_actf[f]], writes=[bb], signal=(f == NF - 1 or t == nt - 1))
                for t in range(nt):
                    bk, bb = accs[t]
                    E("dve", lambda e, bk=bk, t=t, half=half: e.tensor_tensor(
                        out=xt(t)[:, half * 512:(half + 1) * 512], in0=bk[:, :], in1=xt(t)[:, half * 512:(half + 1) * 512],
                        op=ALU.add), reads=[bb, bx[t]], writes=[bx[t]])
                if half == 0:
                    yield "half0"

            rmsnorm_stats_all(nt, 8, xt, bx)
            for t in range(nt):
                yi = t % 2
                E("dve", lambda e, t=t, yi=yi: e.scalar_tensor_tensor(
                    out=YO[yi], in0=xt(t), scalar=SS[:, 8 + t:9 + t], in1=G3[:], op0=ALU.mult, op1=ALU.mult),
                  reads=[bx[t], b_ss3[2], b_G], writes=[b_yo[yi]])
                if kind == "p":
                    dst = yp[bi * 512 + t * 128: bi * 512 + (t + 1) * 128, :]
                else:
                    dst = ys
                E("pool", lambda e, dst=dst, yi=yi: e.dma_start(out=dst, in_=YO[yi]), reads=[b_yo[yi]], chan=out_ch[yi])

        def run_to(g, label):
            while True:
                got = next(g, "end")
                if got == label or got == "end":
                    assert got == label, (got, label)
                    return

        g0 = do_block("s", 0)
        run_to(g0, "uproj")
        for k in range(4):
            stage_cast(WGLU[:, k, :], wglu_d[k * 128:(k + 1) * 128, :], None, [b_wglu])
        handover([b_cst], [b_tmp])
        prep_part1()
        run_to(g0, "end")
        gens = [do_block("p", i) for i in range(4)]
        run_to(gens[0], "headdone")
        for i in range(4):
            nxt = gens[i + 1] if i + 1 < 4 else None
            run_to(gens[i], "convdone")
            if nxt is not None:
                run_to(nxt, "stats")
            run_to(gens[i], "wodone")
            if nxt is not None:
                run_to(nxt, "headpe")
            run_to(gens[i], "n2done")
            for q, f in enumerate((3, 8, 13, 18)):
                run_to(gens[i], "gu_f%d" % f)
                if nxt is not None:
                    run_to(nxt, "l2_%d" % q)
            if nxt is not None:
                run_to(nxt, "headdone")
            run_to(gens[i], "end")

        for c_ in out_ch + [osc_ch, ost_ch]:
            P.final_wait("pool", c_)
        P.replay()
    return nc


_NC_CACHE = {}


def kernel(x_prompt, x_sample, cache_conv, state_ssm_re, state_ssm_im,
           norm_mix, w_in, conv_w, ssm_lam_re, ssm_lam_im, ssm_log_dt,
           ssm_b_re, ssm_b_im, ssm_c_re, ssm_c_im, ssm_d, w_glu, b_glu, w_out,
           norm_ffn, w_gate, w_up, w_down, norm_final):
    f = lambda a: np.ascontiguousarray(np.asarray(a, dtype=np.float32))
    if "nc" not in _NC_CACHE:
        _NC_CACHE["nc"] = build_nc()
    nc = _NC_CACHE["nc"]
    shared = {
        "g1": f(norm_mix).reshape(D), "g2": f(norm_ffn).reshape(D), "g3": f(norm_final).reshape(D),
        "w_in": f(w_in).reshape(D, 2048), "convw": f(conv_w).reshape(3, 512),
        "lamr": f(ssm_lam_re).reshape(16, 128), "lami": f(ssm_lam_im).reshape(16, 128),
        "ldt": f(ssm_log_dt).reshape(16, 2),
        "br": f(ssm_b_re).reshape(32, 64, 16), "bi": f(ssm_b_im).reshape(32, 64, 16),
        "cr": f(ssm_c_re).reshape(32, 16, 64), "ci": f(ssm_c_im).reshape(32, 16, 64),
        "dsk": f(ssm_d).reshape(512), "wglu": f(w_glu).reshape(512, 512), "bglu": f(b_glu).reshape(512),
        "wout": f(w_out).reshape(D, D), "wg": f(w_gate).reshape(D, DFF), "wu": f(w_up).reshape(D, DFF),
        "wd": f(w_down).reshape(DFF, D),
    }
    xp = f(x_prompt)
    xs = f(x_sample)
    cc = f(cache_conv)
    hr = f(state_ssm_re)
    hi = f(state_ssm_im)
    in_maps = []
    for c in range(NCORES):
        m = dict(shared)
        m["xp"] = xp[c]
        m["xs"] = xs[c * 16:(c + 1) * 16].reshape(128, D)
        m["cc"] = cc[0, c * 16:(c + 1) * 16].reshape(16, 1024)
        m["h0r"] = hr[0, c * 16:(c + 1) * 16].reshape(16, 2048)
        m["h0i"] = hi[0, c * 16:(c + 1) * 16].reshape(16, 2048)
        in_maps.append(m)
    res = run_bass_kernel_spmd(nc, in_maps, core_ids=list(range(NCORES)))
    rs = res.results
    y_prompt = np.stack([r["yp"] for r in rs]).reshape(8, 2048, D)
    y_sample = np.concatenate([r["ys"].reshape(16, 8, D) for r in rs], axis=0)
    conv_p = np.stack([r["convp"] for r in rs]).reshape(1, 8, 2, 512)
    re_p = np.stack([r["rep"].reshape(32, 64) for r in rs]).reshape(1, 8, 32, 64)
    im_p = np.stack([r["imp"].reshape(32, 64) for r in rs]).reshape(1, 8, 32, 64)
    conv_s = np.concatenate([r["convs"].reshape(16, 2, 512) for r in rs], axis=0).reshape(1, 128, 2, 512)
    re_s = np.concatenate([r["res"].reshape(16, 32, 64) for r in rs], axis=0).reshape(1, 128, 32, 64)
    im_s = np.concatenate([r["ims"].reshape(16, 32, 64) for r in rs], axis=0).reshape(1, 128, 32, 64)
    return (y_prompt.astype(np.float32), y_sample.astype(np.float32), conv_p.astype(np.float32),
            re_p.astype(np.float32), im_p.astype(np.float32), conv_s.astype(np.float32),
            re_s.astype(np.float32), im_s.astype(np.float32))
```

```python
import math
from contextlib import ExitStack

import numpy as np
import concourse.bass as bass
import concourse.mybir as mybir
from concourse.bass_utils import run_bass_kernel_spmd

F32 = mybir.dt.float32
BF16 = mybir.dt.bfloat16
AF = mybir.ActivationFunctionType
ALU = mybir.AluOpType

ENGS = ("pe", "act", "dve", "pool", "sp")
NCORES = 8
D = 1024
DFF = 2816
NF = 22
EPS = 1e-5
PI = math.pi
TWO_PI = 2.0 * math.pi


class Buf:
    __slots__ = ("name", "w", "r")

    def __init__(self, name):
        self.name = name
        self.w = None
        self.r = {}


class Chan:
    def __init__(self, key):
        self.key = key
        self.count = 0


class Prog:
    def __init__(self, nc, stack):
        self.nc = nc
        self.stack = stack
        self.ops = {e: [] for e in ENGS}
        self.cnt = {e: 0 for e in ENGS}
        self.waited = {e: {} for e in ENGS}
        self.sem = {}
        for e in ENGS:
            self.sem[e] = stack.enter_context(nc.semaphore("sem_" + e))
        self.nchan = 0
        self.finals = []
        self.final_chans = {}

    def chan(self, final=False):
        key = "ch%d" % self.nchan
        self.nchan += 1
        self.sem[key] = self.stack.enter_context(self.nc.semaphore("sem_" + key))
        c = Chan(key)
        if final:
            self.final_chans[key] = c
        return c

    def emit(self, eng, fn, reads=(), writes=(), chan=None, signal=True):
        deps = {}

        def add(k, v, src):
            if k not in deps or deps[k][0] < v:
                deps[k] = (v, src)

        raw = {}
        for b in reads:
            if b.w is not None:
                add(*b.w)
                k_, v_, s_ = b.w
                if k_ not in raw or raw[k_] < v_:
                    raw[k_] = v_
        for b in writes:
            if b.w is not None:
                add(*b.w)
            for k, (v, src) in b.r.items():
                add(k, v, src)
        waits = []
        for k, (v, src) in deps.items():
            if src == "pe" and eng == "pe":
                continue
            if chan is not None and k == chan.key and k in self.final_chans:
                continue
            if self.waited[eng].get(k, 0) >= v:
                continue
            self.waited[eng][k] = v
            waits.append((k, v))
        if chan is not None:
            chan.count += 16
            tok = (chan.key, chan.count, "dma")
            inc = (chan.key, 16)
        elif signal:
            self.cnt[eng] += 1
            tok = (eng, self.cnt[eng], eng)
            inc = (eng, 1)
        else:
            tok = (eng, self.cnt[eng] + 1, eng)
            inc = None
        for b in reads:
            k, v, src = tok
            if k not in b.r or b.r[k][0] < v:
                b.r[k] = (v, src)
        for b in writes:
            b.w = tok
            b.r = {}
        self.ops[eng].append((waits, fn, inc))
        return tok

    def final_wait(self, eng, chan):
        self.finals.append((eng, chan))

    def replay(self):
        nc = self.nc
        with nc.Block() as block:
            def run(engname):
                def body(e):
                    for waits, fn, inc in self.ops[engname]:
                        for k, v in waits:
                            if k in self.final_chans:
                                v = self.final_chans[k].count
                            e.wait_ge(self.sem[k], v)
                        ins = fn(e)
                        if inc is not None:
                            ins.then_inc(self.sem[inc[0]], inc[1])
                    for en, ch in self.finals:
                        if en == engname and ch.count > 0:
                            e.wait_ge(self.sem[ch.key], ch.count)
                return body

            block.tensor(run("pe"))
            block.scalar(run("act"))
            block.vector(run("dve"))
            block.gpsimd(run("pool"))
            block.sync(run("sp"))


def build_nc():
    nc = bass.Bass("TRN2", target_bir_lowering=False)
    dt_in = lambda n, s: nc.dram_tensor(n, s, F32, kind="ExternalInput")
    dt_out = lambda n, s: nc.dram_tensor(n, s, F32, kind="ExternalOutput")
    xp = dt_in("xp", [2048, D]).ap()
    xs = dt_in("xs", [128, D]).ap()
    cc = dt_in("cc", [16, 1024]).ap()
    h0r = dt_in("h0r", [16, 2048]).ap()
    h0i = dt_in("h0i", [16, 2048]).ap()
    g1_d = dt_in("g1", [D]).ap()
    g2_d = dt_in("g2", [D]).ap()
    g3_d = dt_in("g3", [D]).ap()
    w_in_d = dt_in("w_in", [D, 2048]).ap()
    convw_d = dt_in("convw", [3, 512])
    lamr_d = dt_in("lamr", [16, 128]).ap()
    lami_d = dt_in("lami", [16, 128]).ap()
    ldt_d = dt_in("ldt", [16, 2])
    br_d = dt_in("br", [32, 64, 16])
    bi_d = dt_in("bi", [32, 64, 16])
    cr_d = dt_in("cr", [32, 16, 64])
    ci_d = dt_in("ci", [32, 16, 64])
    dsk_d = dt_in("dsk", [512]).ap()
    wglu_d = dt_in("wglu", [512, 512]).ap()
    bglu_d = dt_in("bglu", [512]).ap()
    wout_d = dt_in("wout", [D, D]).ap()
    wg_d = dt_in("wg", [D, DFF]).ap()
    wu_d = dt_in("wu", [D, DFF]).ap()
    wd_d = dt_in("wd", [DFF, D]).ap()

    yp = dt_out("yp", [2048, D]).ap()
    ys = dt_out("ys", [128, D]).ap()
    convp_o = dt_out("convp", [2, 512]).ap()
    rep_o = dt_out("rep", [16, 128]).ap()
    imp_o = dt_out("imp", [16, 128]).ap()
    convs_o = dt_out("convs", [16, 1024]).ap()
    res_o = dt_out("res", [16, 2048]).ap()
    ims_o = dt_out("ims", [16, 2048]).ap()

    win_s = nc.dram_tensor("win_s", [16, 128, 8, 128], BF16, kind="Internal").ap()
    wgu_s = nc.dram_tensor("wgu_s", [NF, 128, 2, 8, 128], BF16, kind="Internal").ap()
    wd_s = nc.dram_tensor("wd_s", [NF, 128, D], BF16, kind="Internal").ap()
    wout_s = nc.dram_tensor("wout_s", [8, 128, D], BF16, kind="Internal").ap()

    with ExitStack() as st, nc.allow_non_contiguous_dma(reason="small param layouts"):
        P = Prog(nc, st)
        sb = lambda n, s, d=F32: st.enter_context(nc.sbuf_tensor(n, s, d))
        E = P.emit

        banks = [st.enter_context(nc.psum_tensor("bank%d" % i, [128, 512], F32)) for i in range(8)]
        bbufs = [Buf("bank%d" % i) for i in range(8)]
        bank_i = [0]

        pinned = set()

        def next_bank(pin=False):
            while True:
                i = bank_i[0] % 8
                bank_i[0] += 1
                if i not in pinned:
                    break
            if pin:
                pinned.add(i)
            return banks[i], bbufs[i]

        def unpin(bbuf):
            pinned.discard(bbufs.index(bbuf))

        XB = sb("XB", [128, 4, D]); b_xb = [Buf("xb%d" % t) for t in range(4)]
        SETUPT = sb("SETUPT", [128, 6656])
        ARENA = sb("ARENA", [128, 5632])
        STG = [sb("STG%d" % i, [128, 1024]) for i in range(3)]
        b_stgs = [Buf("stg%d" % i) for i in range(3)]
        OSC = sb("OSC", [16, 1024]); b_osc = Buf("OSC")
        CCN = OSC

        def handover(olds, news):
            for n in news:
                for o in olds:
                    if o.w is not None and n.r.get(o.w[0], (0, ""))[0] < o.w[1]:
                        n.r[o.w[0]] = (o.w[1], o.w[2])
                    for k_, (v_, s_) in o.r.items():
                        if n.r.get(k_, (0, ""))[0] < v_:
                            n.r[k_] = (v_, s_)

        xin_ch = [P.chan() for _ in range(8)]
        E("sp", lambda e: e.dma_start(out=XB[:, 0, :], in_=xs), writes=[b_xb[0]], chan=xin_ch[0])

        ld = P.chan(final=True)
        ldm = P.chan(final=True)
        ldc = P.chan(final=True)
        ones = sb("ones", [128, 128]); b_ones = Buf("ones")
        identf = sb("identf", [128, 128]); b_identf = Buf("identf")
        identb = sb("identb", [128, 128], BF16); b_identb = Buf("identb")
        E("pool", lambda e: e.memset(ones[:], 1.0), writes=[b_ones])
        E("pool", lambda e: e.affine_select(out=identf[:], in_=ones[:], pattern=[[-1, 128]],
                                            compare_op=ALU.is_equal, fill=0.0, base=0,
                                            channel_multiplier=1), reads=[b_ones], writes=[b_identf])
        E("pool", lambda e: e.tensor_copy(out=identb[:], in_=identf[:]), reads=[b_identf], writes=[b_identb])

        G3 = sb("G3", [128, D]); G1C = sb("G1C", [128, 8]); G2C = sb("G2C", [128, 8])
        b_G = Buf("G")
        CW = sb("CW", [128, 4, 3]); DV = sb("DV", [128, 4]); BG = sb("BG", [128, 4])
        LDT = sb("LDT", [128, 16])
        BR = sb("BR", [128, 16, 16]); BI = sb("BI", [128, 16, 16])
        b_par = Buf("params")
        b_parm = Buf("params_m")
        b_cst = Buf("CST")
        b_tmp = Buf("ssm_tmp")
        b_bd = Buf("BD")
        b_bl = [Buf("chain0"), Buf("chain1")]
        ldg = P.chan(final=True)
        E("sp", lambda e: e.dma_start(out=G1C[:], in_=g1_d.rearrange("(k p) -> p k", p=128)), writes=[b_G], chan=ldg)
        E("sp", lambda e: e.dma_start(out=G2C[:], in_=g2_d.rearrange("(k p) -> p k", p=128)), writes=[b_G], chan=ldg)
        E("sp", lambda e: e.dma_start(out=G3[:], in_=g3_d.partition_broadcast(128)), writes=[b_G], chan=ldg)
        for kk in range(3):
            E("pool", lambda e, kk=kk: e.dma_start(out=CW[:, :, kk], in_=bass.AP(convw_d, kk * 512, [[1, 128], [128, 4]])),
              writes=[b_parm], chan=ldm)
        E("pool", lambda e: e.dma_start(out=DV[:], in_=dsk_d.rearrange("(j p) -> p j", p=128)), writes=[b_parm], chan=ldm)
        E("pool", lambda e: e.dma_start(out=BG[:], in_=bglu_d.rearrange("(j p) -> p j", p=128)), writes=[b_parm], chan=ldm)
        PARN = sb("PARN", [32, 128]); b_parn = Buf("PARN")
        LRLI = sb("LRLI", [128, 32]); b_lr = Buf("LRLI")
        LR = LRLI[:, 0:16]; LI = LRLI[:, 16:32]
        b_ldt = Buf("LDT")
        ldl = P.chan(final=True)
        E("sp", lambda e: e.dma_start(out=PARN[0:16, :], in_=lamr_d), writes=[b_parn], chan=ldl)
        E("sp", lambda e: e.dma_start(out=PARN[16:32, :], in_=lami_d), writes=[b_parn], chan=ldl)
        ldt_ch = P.chan(final=True)
        for ee in range(2):
            E("sp", lambda e, ee=ee: e.dma_start(out=LDT[ee * 64:(ee + 1) * 64, :],
                                               in_=bass.AP(ldt_d, ee, [[0, 64], [2, 16]])),
              writes=[b_ldt], chan=ldt_ch)
        bk_, bb_ = next_bank()
        E("pe", lambda e: e.matmul(bk_[:, 0:32], lhsT=PARN[:, :], rhs=identf[0:32, 0:32], start=True, stop=True),
          reads=[b_parn, b_identf], writes=[bb_])
        E("dve", lambda e: e.tensor_copy(out=LRLI[:, :], in_=bk_[:, 0:32]), reads=[bb_], writes=[b_lr])
        for ee in range(2):
            E("act", lambda e, ee=ee: e.dma_start(out=BR[ee * 64:(ee + 1) * 64, :, :],
                                                 in_=bass.AP(br_d, ee * 1024, [[16, 64], [2048, 16], [1, 16]])),
              writes=[b_par], chan=ld)
            E("act", lambda e, ee=ee: e.dma_start(out=BI[ee * 64:(ee + 1) * 64, :, :],
                                                 in_=bass.AP(bi_d, ee * 1024, [[16, 64], [2048, 16], [1, 16]])),
              writes=[b_par], chan=ld)

        b_ssm = Buf("ssmprep")
        sv = lambda a, b_: SETUPT[:, a:b_]
        v16 = lambda a: SETUPT[:, a:a + 256].rearrange("p (a h) -> p a h", h=16)
        v32 = lambda a: SETUPT[:, a:a + 512].rearrange("p (a h) -> p a h", h=32)
        BDR = v32(0); NBDI = v32(512)
        CHN = [(v32(1024), v32(1536)), (v32(2048), v32(2560))]
        BBR = v16(3072); BBI = v16(3328); CRT = v16(3584); CIT = v16(3840)
        K0F = SETUPT[:, 4096:4608].rearrange("p (j c) -> p j c", c=128)
        TA = v32(4608); TB = v32(5120); TC = v32(5632); TD = v32(6144)
        U0 = v16(6144); U1 = v16(6400)
        CST = SETUPT[0:64, 4608:5632].rearrange("p (r j c) -> p r j c", r=2, j=4)
        for ri, cd in enumerate((cr_d, ci_d)):
            for j in range(4):
                for k in range(4):
                    pr = 4 * j + k
                    E("pool", lambda e, ri=ri, cd=cd, j=j, k=k, pr=pr: e.dma_start(
                        out=CST[k * 16:(k + 1) * 16, ri, j, :].rearrange("p (e q) -> p e q", e=2),
                        in_=bass.AP(cd, 2 * pr * 1024, [[64, 16], [1024, 2], [1, 64]])),
                      writes=[b_cst], chan=ldc)
        E("pool", lambda e: e.dma_start(out=CCN[:], in_=cc), writes=[b_osc], chan=ldm)

        WGLU = sb("WGLU", [128, 4, 512], BF16); b_wglu = Buf("WGLU")
        stg_ch = [P.chan() for _ in range(3)]
        stg_i = [0]
        cast_i = [0]
        scr_bufs = {}

        def stage_cast(dst_ap, src_ap, gain, wr_bufs, force_act=False, kbufs=None):
            i = stg_i[0] % 3
            stg_i[0] += 1
            st_t = STG[i]
            if len(src_ap.shape) == 3:
                view = st_t[:, 0:src_ap.shape[1] * src_ap.shape[2]].rearrange("p (k c) -> p k c", c=src_ap.shape[2])
            else:
                view = st_t[:, 0:src_ap.shape[1]]
            E("sp", lambda e: e.dma_start(out=view, in_=src_ap), writes=[b_stgs[i]], chan=stg_ch[i])
            on_act = force_act or (cast_i[0] % 2 == 0)
            cast_i[0] += 1
            rd = [b_stgs[i], b_G]
            if gain is None:
                if on_act:
                    E("act", lambda e: e.copy(out=dst_ap, in_=view), reads=rd, writes=wr_bufs)
                else:
                    E("dve", lambda e: e.tensor_copy(out=dst_ap, in_=view), reads=rd, writes=wr_bufs)
            else:
                if on_act:
                    for k in range(8):
                        E("act", lambda e, k=k: e.mul(out=dst_ap[:, k, :], in_=view[:, k, :], mul=gain[:, k:k + 1]),
                          reads=rd, writes=([kbufs[k]] if kbufs is not None else wr_bufs))
                else:
                    E("dve", lambda e: e.tensor_tensor(out=dst_ap, in0=view,
                                                       in1=gain.unsqueeze(2).to_broadcast([128, 8, 128]), op=ALU.mult),
                      reads=rd, writes=(kbufs if kbufs is not None else wr_bufs))

        class WStream:
            def __init__(self, name, n, shape):
                self.name = name
                self.n = n
                self.t = [sb("%s%d" % (name, i), shape, BF16) for i in range(n)]
                nsub = 16 if name == "RGU" else (8 if name == "RWIN" else 1)
                self.b = [[Buf("%s%d_%d" % (name, i, q)) for q in range(nsub)] for i in range(n)]
                self.ch = [P.chan() for _ in range(n)]
                self.sch = [P.chan() for _ in range(n)]
                self.i = 0

            def get(self, first, key, parts_fn, scratch_ap):
                i = self.i % self.n
                self.i += 1
                t = self.t[i]
                if first:
                    for pi, (dst_ap, src_ap, gain) in enumerate(parts_fn(t)):
                        kb = self.b[i][pi * 8:(pi + 1) * 8] if gain is not None else None
                        stage_cast(dst_ap, src_ap, gain, self.b[i], force_act=(self.name == "RWIN"), kbufs=kb)
                    sbuf = Buf("scr_%s_%s" % (self.name, key))
                    scr_bufs[(self.name, key)] = sbuf
                    E("pool", lambda e: e.dma_start(out=scratch_ap, in_=t[:]), reads=self.b[i], writes=[sbuf],
                      chan=self.sch[i])
                else:
                    E("sp", lambda e: e.dma_start(out=t[:], in_=scratch_ap), reads=[scr_bufs[(self.name, key)]],
                      writes=self.b[i], chan=self.ch[i])
                return t, self.b[i]


        sm = lambda n, s: sb(n, s)
        DT = sm("DT", [128, 16]); AA = sm("AA", [128, 16]); TH = sm("TH", [128, 16]); MAG = sm("MAG", [128, 16])
        T0 = sm("T0", [128, 16]); T1 = sm("T1", [128, 16]); T2 = sm("T2", [128, 16])
        QR = sm("QR", [128, 16]); QI = sm("QI", [128, 16])
        L1R = sm("L1R", [128, 16]); L1I = sm("L1I", [128, 16])
        L8R = sm("L8R", [128, 16]); L8I = sm("L8I", [128, 16])
        RR = sm("RR", [128, 16]); TH8 = sm("TH8", [128, 16])
        COST = sm("COST", [128, 16, 64]); SINT = sm("SINT", [128, 16, 64]); RTAB = sm("RTAB", [128, 16, 64])
        IOT = sm("IOT", [128, 64])
        L2T = sb("L2T", [128, 3, 2, 256])
        b_l2 = Buf("l2")
        L2F = L2T[:, :, :, :].rearrange("p a b c -> p (a b c)")
        TI = st.enter_context(nc.sbuf_tensor("TI", [128, 16], mybir.dt.int32))
        TIB = st.enter_context(nc.sbuf_tensor("TIB", [128, 512], mybir.dt.int32))
        TF = sm("TF", [128, 16]); TY = sm("TY", [128, 16]); TM = sm("TM", [128, 16])
        KT = sb("KT", [128, 4, 8, 128], BF16)
        BLT = sb("BLT", [128, 4, 8, 2, 128], BF16)
        CH = sb("CH", [128, 16, 8, 2, 32], BF16)
        b_kt = Buf("KT"); b_blt = Buf("BLT"); b_ch = Buf("CH")

        deferred = [None]

        def V(fn, eng="dve", extra=()):
            if deferred[0] is not None:
                deferred[0].append(lambda: E(eng, fn, reads=[b_ssm, b_lr] + list(extra), writes=[b_ssm] + list(extra)))
            else:
                E(eng, fn, reads=[b_ssm, b_lr] + list(extra), writes=[b_ssm] + list(extra))

        def tt(o, a, b, op, eng="dve", extra=()):
            V(lambda e: e.tensor_tensor(out=o, in0=a, in1=b, op=op), eng, extra)

        def ts(o, a, s1, op0, s2=None, op1=None, eng="dve", extra=()):
            if op1 is None:
                V(lambda e: e.tensor_scalar(out=o, in0=a, scalar1=s1, scalar2=None, op0=op0), eng, extra)
            else:
                V(lambda e: e.tensor_scalar(out=o, in0=a, scalar1=s1, scalar2=s2, op0=op0, op1=op1), eng, extra)

        def act(o, a, func, scale=1.0, extra=()):
            V(lambda e: e.activation(out=o, in_=a, func=func, scale=scale), "act", extra)

        def range_reduce(dst, src, offset, y, ti, tf, tm, extra=()):
            ts(y, src, offset, ALU.add, extra=extra)
            ts(tf, y, 1.0 / TWO_PI, ALU.mult, extra=extra)
            V(lambda e: e.tensor_copy(out=ti, in_=tf), extra=extra)
            V(lambda e: e.tensor_copy(out=tf, in_=ti), extra=extra)
            V(lambda e: e.scalar_tensor_tensor(out=dst, in0=tf, scalar=-TWO_PI, in1=y, op0=ALU.mult, op1=ALU.add),
              extra=extra)
            ts(tm, dst, PI, ALU.is_gt, TWO_PI, ALU.mult, extra=extra)
            tt(dst, dst, tm, ALU.subtract, extra=extra)
            ts(tm, dst, -PI, ALU.is_lt, TWO_PI, ALU.mult, extra=extra)
            tt(dst, dst, tm, ALU.add, extra=extra)

        def Vx(fn, rd, wr, eng="dve"):
            E(eng, fn, reads=rd, writes=wr)

        def cmul(dst, src, cr, ci, b_dst, b_src):
            (dr, di), (sr, si) = dst, src
            for o_, a_, c_ in ((TA, sr, cr), (TB, si, ci), (TC, sr, ci), (TD, si, cr)):
                Vx(lambda e, o_=o_, a_=a_, c_=c_: e.tensor_tensor(out=o_, in0=a_, in1=c_, op=ALU.mult),
                   [b_src, b_ssm], [b_tmp])
            Vx(lambda e: e.tensor_tensor(out=dr, in0=TA, in1=TB, op=ALU.subtract), [b_tmp], [b_dst])
            Vx(lambda e: e.tensor_tensor(out=di, in0=TC, in1=TD, op=ALU.add), [b_tmp], [b_dst])

        def prep_part1():
            act(DT[:], LDT[:], AF.Exp, extra=[b_ldt])
            tt(AA[:], LR, DT[:], ALU.mult)
            tt(TH[:], LI, DT[:], ALU.mult)
            act(MAG[:], AA[:], AF.Exp)
            range_reduce(T0[:], TH[:], 0.0, TY[:], TI[:], TF[:], TM[:])
            act(T1[:], T0[:], AF.Sin)
            range_reduce(T0[:], TH[:], 0.5 * PI, TY[:], TI[:], TF[:], TM[:])
            act(T2[:], T0[:], AF.Sin)
            tt(L1R[:], MAG[:], T2[:], ALU.mult)
            tt(L1I[:], MAG[:], T1[:], ALU.mult)
            ts(T0[:], L1R[:], -1.0, ALU.add)
            tt(T1[:], LR, LR, ALU.mult)
            tt(T2[:], LI, LI, ALU.mult)
            tt(T1[:], T1[:], T2[:], ALU.add)
            V(lambda e: e.reciprocal(out=T1[:], in_=T1[:]))
            tt(QR[:], T0[:], LR, ALU.mult)
            tt(T2[:], L1I[:], LI, ALU.mult)
            tt(QR[:], QR[:], T2[:], ALU.add)
            tt(QR[:], QR[:], T1[:], ALU.mult)
            tt(QI[:], L1I[:], LR, ALU.mult)
            tt(T2[:], T0[:], LI, ALU.mult)
            tt(QI[:], QI[:], T2[:], ALU.subtract)
            tt(QI[:], QI[:], T1[:], ALU.mult)
            bc = lambda t: t.unsqueeze(2).to_broadcast([128, 16, 16])
            x_ = [b_tmp, b_par]
            tt(U0, BR[:], bc(QR[:]), ALU.mult, extra=x_)
            tt(U1, BI[:], bc(QI[:]), ALU.mult, extra=x_)
            tt(BBR, U0, U1, ALU.subtract, extra=x_)
            tt(U0, BI[:], bc(QR[:]), ALU.mult, extra=x_)
            tt(U1, BR[:], bc(QI[:]), ALU.mult, extra=x_)
            tt(BBI, U0, U1, ALU.add, extra=x_)
            (r0, i0) = CHN[0]
            Vx(lambda e: e.memset(BDR, 0.0), [], [b_bd], "pool")
            Vx(lambda e: e.memset(NBDI, 0.0), [], [b_bd], "pool")
            Vx(lambda e: e.memset(r0, 0.0), [], [b_bl[0]], "pool")
            Vx(lambda e: e.memset(i0, 0.0), [], [b_bl[0]], "pool")
            for ci_ in range(2):
                Vx(lambda e, ci_=ci_: e.memset(CHN[1][ci_], 0.0), [], [b_bl[1]], "pool")
            for ee in range(2):
                ps_ = slice(ee * 64, (ee + 1) * 64)
                cs_ = slice(ee * 16, (ee + 1) * 16)
                Vx(lambda e, ps_=ps_, cs_=cs_: e.tensor_copy(out=BDR[ps_, :, cs_], in_=BBR[ps_, :, :]), [b_ssm], [b_bd])
                Vx(lambda e, ps_=ps_, cs_=cs_: e.tensor_copy(out=r0[ps_, :, cs_], in_=BBR[ps_, :, :]), [b_ssm], [b_bl[0]])
                Vx(lambda e, ps_=ps_, cs_=cs_: e.tensor_copy(out=i0[ps_, :, cs_], in_=BBI[ps_, :, :]), [b_ssm], [b_bl[0]])
                Vx(lambda e, ps_=ps_, cs_=cs_: e.tensor_scalar(out=NBDI[ps_, :, cs_], in0=BBI[ps_, :, :], scalar1=-1.0,
                                                               scalar2=None, op0=ALU.mult), [b_ssm], [b_bd])
            l1r = L1R[:].unsqueeze(2).to_broadcast([128, 16, 32])
            l1i = L1I[:].unsqueeze(2).to_broadcast([128, 16, 32])
            cur = 0
            for s in range(7, -1, -1):
                if s > 0:
                    cmul(CHN[1 - cur], CHN[cur], l1r, l1i, b_bl[1 - cur], b_bl[cur])
                for ri in range(2):
                    src = CHN[cur][ri]
                    b_, bbx = next_bank()
                    for j in range(4):
                        E("pe", lambda e, b_=b_, src=src, j=j: e.matmul(
                            b_[:, j * 128:(j + 1) * 128],
                            lhsT=src[:, 4 * j:4 * j + 4, :].rearrange("p k c -> p (k c)"),
                            rhs=identf[:, :], start=True, stop=True),
                          reads=[b_bl[cur], b_identf], writes=[bbx])
                    E("act", lambda e, b_=b_, s=s, ri=ri: e.copy(
                        out=BLT[:, :, s, ri, :], in_=b_[:, :].rearrange("p (j c) -> p j c", j=4)),
                      reads=[bbx, b_blt], writes=[b_blt])
                cur = 1 - cur
            V(lambda e: e.tensor_copy(out=L8R[:], in_=L1R[:]))
            V(lambda e: e.tensor_copy(out=L8I[:], in_=L1I[:]))
            for _ in range(3):
                tt(T0[:], L8R[:], L8R[:], ALU.mult)
                tt(T1[:], L8I[:], L8I[:], ALU.mult)
                tt(T2[:], L8R[:], L8I[:], ALU.mult)
                tt(L8R[:], T0[:], T1[:], ALU.subtract)
                ts(L8I[:], T2[:], 2.0, ALU.mult)

        def prep_part2():
            act(RR[:], AA[:], AF.Exp, scale=8.0)
            ts(T0[:], TH[:], 8.0, ALU.mult)
            range_reduce(TH8[:], T0[:], 0.0, TY[:], TI[:], TF[:], TM[:])
            V(lambda e: e.iota(IOT[:], pattern=[[1, 64]], base=1, channel_multiplier=0,
                               allow_small_or_imprecise_dtypes=True), "pool")
            for hh in range(2):
                hs_ = slice(hh * 8, hh * 8 + 8)
                ang = L2F[:, 0:512].rearrange("p (a c) -> p a c", c=64)
                ay = L2F[:, 512:1024].rearrange("p (a c) -> p a c", c=64)
                am = L2F[:, 1024:1536].rearrange("p (a c) -> p a c", c=64)
                ai = TIB[:, :].rearrange("p (a c) -> p a c", c=64)
                x_ = [b_l2]
                tt(ang, TH8[:, hs_].unsqueeze(2).to_broadcast([128, 8, 64]),
                   IOT[:].unsqueeze(1).to_broadcast([128, 8, 64]), ALU.mult, extra=x_)
                for off_, dstt in ((0.0, SINT), (0.5 * PI, COST)):
                    ts(ay, ang, off_, ALU.add, extra=x_)
                    ts(dstt[:, hs_, :], ay, 1.0 / TWO_PI, ALU.mult, extra=x_)
                    V(lambda e, dstt=dstt, hs_=hs_, ai=ai: e.tensor_copy(out=ai, in_=dstt[:, hs_, :]), extra=x_)
                    V(lambda e, dstt=dstt, hs_=hs_, ai=ai: e.tensor_copy(out=dstt[:, hs_, :], in_=ai), extra=x_)
                    V(lambda e, dstt=dstt, hs_=hs_, ay=ay: e.scalar_tensor_tensor(
                        out=ay, in0=dstt[:, hs_, :], scalar=-TWO_PI, in1=ay, op0=ALU.mult, op1=ALU.add), extra=x_)
                    ts(am, ay, PI, ALU.is_gt, TWO_PI, ALU.mult, extra=x_)
                    tt(ay, ay, am, ALU.subtract, extra=x_)
                    ts(am, ay, -PI, ALU.is_lt, TWO_PI, ALU.mult, extra=x_)
                    tt(ay, ay, am, ALU.add, extra=x_)
                    act(dstt[:, hs_, :], ay, AF.Sin, extra=x_)
            for k in range(4):
                V(lambda e, k=k: e.tensor_copy(out=RTAB[:, 4 * k:4 * k + 4, :],
                                               in_=RR[:, k:16:4].unsqueeze(2).to_broadcast([128, 4, 64])))
            V(lambda e: e.memset(RTAB[:, :, 0:1], 0.0))

        def prep_part3():
            E("pool", lambda e: e.memset(KT[:], 0.0), writes=[b_kt])
            V(lambda e: e.memset(K0F, 0.0), "pool")
            for ci_ in range(2):
                for cj_ in range(2):
                    Vx(lambda e, ci_=ci_, cj_=cj_: e.memset(CHN[ci_][cj_], 0.0), [], [b_bl[ci_]], "pool")
            (r0, i0) = CHN[0]
            for ee in range(2):
                ps_ = slice(ee * 64, (ee + 1) * 64)
                cs_ = slice(ee * 16, (ee + 1) * 16)
                Vx(lambda e, ps_=ps_, cs_=cs_: e.tensor_copy(out=r0[ps_, :, cs_], in_=CRT[ps_, :, :]), [b_ssm], [b_bl[0]])
                Vx(lambda e, ps_=ps_, cs_=cs_: e.tensor_copy(out=i0[ps_, :, cs_], in_=CIT[ps_, :, :]), [b_ssm], [b_bl[0]])
            l1r = L1R[:].unsqueeze(2).to_broadcast([128, 16, 32])
            l1i = L1I[:].unsqueeze(2).to_broadcast([128, 16, 32])
            cur = 0
            for tau in range(9):
                c_re, c_im = CHN[cur]
                if tau < 8:
                    cmul(CHN[1 - cur], CHN[cur], l1r, l1i, b_bl[1 - cur], b_bl[cur])
                if tau >= 1:
                    s = tau - 1
                    E("act", lambda e, s=s, c_re=c_re: e.copy(out=CH[:, :, s, 0, :], in_=c_re), reads=[b_bl[cur]], writes=[b_ch])
                    E("act", lambda e, s=s, c_im=c_im: e.mul(out=CH[:, :, s, 1, :], in_=c_im, mul=-1.0), reads=[b_bl[cur]], writes=[b_ch])
                if tau <= 7:
                    bk, bb = next_bank()
                    for pr in range(16):
                        j, k = divmod(pr, 4)
                        o = bk[32 * k:32 * k + 32, j * 32:(j + 1) * 32]
                        E("pe", lambda e, o=o, pr=pr, k=k, c_re=c_re: e.matmul(
                            o, lhsT=BDR[:, pr, :], rhs=c_re[:, pr, :], start=True, stop=False,
                            tile_position=(0, 32 * k), skip_group_check=True),
                          reads=[b_bd, b_bl[cur]], writes=[bb], signal=False)
                        E("pe", lambda e, o=o, pr=pr, k=k, c_im=c_im: e.matmul(
                            o, lhsT=NBDI[:, pr, :], rhs=c_im[:, pr, :], start=False, stop=True,
                            tile_position=(0, 32 * k), skip_group_check=True),
                          reads=[b_bd, b_bl[cur]], writes=[bb], signal=(pr == 15))
                    for k in range(4):
                        pq = slice(32 * k, 32 * k + 32)
                        src_v = bk[pq, 0:128].rearrange("p (j c) -> p j c", j=4)
                        if tau == 0:
                            E("dve", lambda e, pq=pq, src_v=src_v: e.tensor_copy(out=K0F[pq, :, pq], in_=src_v),
                              reads=[bb, b_ssm], writes=[b_ssm])
                        else:
                            E("act", lambda e, pq=pq, src_v=src_v, tau=tau: e.copy(out=KT[pq, :, tau, pq], in_=src_v),
                              reads=[bb, b_kt], writes=[b_kt])
                cur = 1 - cur
            for j in range(4):
                E("dve", lambda e, j=j: e.scalar_tensor_tensor(
                    out=KT[:, j, 0, :], in0=identf[:, :], scalar=DV[:, j:j + 1], in1=K0F[:, j, :],
                    op0=ALU.mult, op1=ALU.add), reads=[b_identf, b_parm, b_ssm, b_kt], writes=[b_kt])

        for ri, ct in enumerate((CRT, CIT)):
            for j in range(4):
                bk, bb = next_bank()
                E("pe", lambda e, bk=bk, ri=ri, j=j: e.matmul(bk[:, 0:64], lhsT=CST[:, ri, j, :], rhs=identf[0:64, 0:64],
                                                              start=True, stop=True),
                  reads=[b_cst, b_identf], writes=[bb])
                E("dve", lambda e, bk=bk, ct=ct, j=j: e.tensor_copy(
                    out=ct[:, 4 * j:4 * j + 4, :], in_=bk[:, 0:64].rearrange("p (k h) -> p k h", k=4)),
                  reads=[bb, b_ssm], writes=[b_ssm])
        H0 = sb("H0", [128, 2, 16, 16]); H0B = sb("H0B", [128, 2, 16, 16], BF16); b_h0 = Buf("H0")
        CCH = sb("CCH", [128, 4, 2, 16]); b_cch = Buf("CCH")
        H0N = XB[0:16, 1:3, :].rearrange("p t d -> p (t d)")
        h0_ch = P.chan()

        def load_h0():
            for ri in range(2):
                src_h = h0r if ri == 0 else h0i
                E("sp", lambda e, src_h=src_h: e.dma_start(out=H0N[:, :], in_=src_h), writes=[b_xb[1], b_xb[2]], chan=h0_ch)
                bk, bb = next_bank()
                for pr in range(16):
                    E("pe", lambda e, bk=bk, pr=pr: e.matmul(
                        bk[:, pr * 16:(pr + 1) * 16], lhsT=H0N[:, pr * 128:(pr + 1) * 128],
                        rhs=identf[0:16, 0:16], start=True, stop=True),
                      reads=[b_xb[1], b_xb[2], b_identf], writes=[bb])
                E("dve", lambda e, bk=bk, ri=ri: e.tensor_copy(
                    out=H0[:, ri, :, :], in_=bk[:, 0:256].rearrange("p (a b) -> p a b", a=16)),
                  reads=[bb], writes=[b_h0])
                E("act", lambda e, bk=bk, ri=ri: e.copy(
                    out=H0B[:, ri, :, :], in_=bk[:, 0:256].rearrange("p (a b) -> p a b", a=16)),
                  reads=[bb], writes=[b_h0])

        bk, bb = next_bank()
        for kk in range(2):
            for j in range(4):
                E("pe", lambda e, bk=bk, kk=kk, j=j: e.matmul(
                    bk[:, (j * 2 + kk) * 16:(j * 2 + kk + 1) * 16],
                    lhsT=CCN[:, kk * 512 + j * 128:kk * 512 + (j + 1) * 128],
                    rhs=identf[0:16, 0:16], start=True, stop=True),
                  reads=[b_osc, b_identf], writes=[bb])
        E("dve", lambda e, bk=bk: e.tensor_copy(
            out=CCH[:], in_=bk[:, 0:128].rearrange("p (j k b) -> p j k b", j=4, k=2)),
          reads=[bb], writes=[b_cch])


        XT = ARENA[:, 0:2048].bitcast(BF16).rearrange("p (k c) -> p k c", c=512); b_xt = Buf("XT")
        SS = sb("SS", [128, 12]); b_ss3 = [Buf("SS0"), Buf("SS1"), Buf("SS2")]
        XC = sb("XC", [128, 512]); b_xc = Buf("xc")
        VX = sb("VX", [128, 4, 528]); b_vx = [Buf("vx%d" % j) for j in range(4)]
        CA = sb("CA", [128, 512]); b_ca = Buf("ca")
        MIX = ARENA[:, 2048:4096].bitcast(BF16).rearrange("p (k c) -> p k c", c=512); b_mix = [Buf("mix%d" % j) for j in range(8)]
        UB = sb("UB", [128, 4, 512], BF16); b_ub = [Buf("ub%d" % j) for j in range(4)]
        WIN = L2T[:, 0, :, :]; WW = L2T[:, 1, :, :]; TT_ = L2T[:, 2, :, :]
        ZB = sb("ZB", [128, 2, 16, 65], BF16); b_zb = Buf("ZB")
        ZP = sb("ZP", [128, 2, 16]); b_zp = Buf("ZP")
        ZI = sb("ZI", [128, 2, 4])
        ZN = sb("ZN", [128, 2, 16, 16]); b_zn = Buf("ZN")
        XNS = [sb("XN0", [128, D], BF16), ZN[:, :, :, :].rearrange("p a b c -> p (a b c)").bitcast(BF16)]
        b_xns = [Buf("xn0"), Buf("xn1")]
        ZG = ARENA[:, 4096:5120].bitcast(BF16).rearrange("p (k c) -> p k c", c=512); b_zg = [Buf("zg%d" % j) for j in range(4)]
        HT = SETUPT[:, 0:2048].bitcast(BF16).rearrange("p (k c) -> p k c", c=512); b_ht = Buf("HT")
        YO = [SETUPT[:, 2048:3072], SETUPT[:, 3072:4096]]; b_yo = [Buf("yo0"), Buf("yo1")]
        SIL = [SETUPT[:, 4096:4608], SETUPT[:, 4608:5120]]; b_sil = [Buf("sil0"), Buf("sil1")]
        GT = [SETUPT[:, 5120:5376].bitcast(BF16), SETUPT[:, 5376:5632].bitcast(BF16)]; b_gt = [Buf("gt0"), Buf("gt1")]
        ACT_ = ARENA[:, :].bitcast(BF16)
        b_actf = [Buf("act%d" % f) for f in range(NF)]
        OST = XB[0:16, 1:3, :].rearrange("p t d -> p (t d)")

        E("dve", lambda e: e.memset(VX[:], 0.0), writes=b_vx)
        E("dve", lambda e: e.memset(ZP[:], 0.0), writes=[b_zp])
        E("dve", lambda e: e.memset(ZB[:], 0.0), writes=[b_zb])

        def lookahead(stream, first, reqs, depth):
            q = [stream.get(first, *reqs[i]) for i in range(min(depth, len(reqs)))]
            for i in range(len(reqs)):
                if i + depth < len(reqs):
                    q.append(stream.get(first, *reqs[i + depth]))
                yield q.pop(0)

        r_win = WStream("RWIN", 2, [128, 8, 128])
        r_gu = WStream("RGU", 2, [128, 2, 8, 128])
        r_w = WStream("RW", 4, [128, 512])
        XTO = sb("XTO", [128, 8, 512], BF16); b_xto = Buf("XTO")
        mixer_bufs = [b_xt] + b_mix + b_zg
        bgc_ch = P.chan(final=True)
        b_wos = [Buf("scr_wo%d" % k) for k in range(8)]
        b_wds = [Buf("scr_wd%d" % f) for f in range(NF)]
        for k in range(8):
            E("pool", lambda e, k=k: e.dma_start(out=wout_s[k, :, :], in_=wout_d[k * 128:(k + 1) * 128, :]),
              writes=[b_wos[k]], chan=bgc_ch)
            for half in range(2):
                scr_bufs[("RW", ("wo", k, half))] = b_wos[k]
        for f in range(NF):
            E("pool", lambda e, f=f: e.dma_start(out=wd_s[f, :, :], in_=wd_d[f * 128:(f + 1) * 128, :]),
              writes=[b_wds[f]], chan=bgc_ch)
            for half in range(2):
                scr_bufs[("RW", ("wd", f, half))] = b_wds[f]
        XB2 = [STG[0][:, :], STG[1][:, :], STG[2][:, :], SETUPT[:, 5632:6656]]
        b_xb2 = [Buf("xb2_%d" % t) for t in range(4)]
        xsel = [(lambda t: XB[:, t, :], b_xb), (lambda t: XB2[t], b_xb2)]
        out_ch = [P.chan(), P.chan()]
        osc_ch = P.chan()
        ost_ch = P.chan()

        def rmsnorm_stats_all(nt, colbase, xt, bx):
            cols = SS[:, colbase:colbase + nt]
            E("dve", lambda e: e.memset(SS[:, colbase:colbase + 4], 0.0), reads=[b_ss3[colbase // 4]], writes=[b_ss3[colbase // 4]])
            for t in range(nt):
                E("act", lambda e, t=t: e.activation(out=XNS[t % 2][:, :], in_=xt(t), func=AF.Square,
                                                     accum_out=SS[:, colbase + t:colbase + t + 1]),
                  reads=[bx[t], b_ss3[colbase // 4]], writes=[b_xns[t % 2], b_ss3[colbase // 4]])
            E("dve", lambda e: e.tensor_scalar(out=cols, in0=cols, scalar1=1.0 / D, scalar2=EPS, op0=ALU.mult,
                                               op1=ALU.add), reads=[b_ss3[colbase // 4]], writes=[b_ss3[colbase // 4]])
            E("act", lambda e: e.activation(out=cols, in_=cols, func=AF.Sqrt), reads=[b_ss3[colbase // 4]], writes=[b_ss3[colbase // 4]])
            E("dve", lambda e: e.reciprocal(out=cols, in_=cols), reads=[b_ss3[colbase // 4]], writes=[b_ss3[colbase // 4]])

        def norm_transpose(nt, DST, b_dst, colbase, xt, bx, do_stats=True):
            if do_stats:
                rmsnorm_stats_all(nt, colbase, xt, bx)
            def scale(t):
                E("dve", lambda e, t=t: e.tensor_scalar(
                    out=XNS[t % 2][:, :], in0=xt(t), scalar1=SS[:, colbase + t:colbase + t + 1], scalar2=None,
                    op0=ALU.mult), reads=[bx[t], b_ss3[colbase // 4]], writes=[b_xns[t % 2]])

            scale(0)
            for t in range(nt):
                if t + 1 < nt:
                    scale(t + 1)
                for hh in range(2):
                    bk, bb = next_bank()
                    for kk in range(4):
                        k = hh * 4 + kk
                        E("pe", lambda e, bk=bk, kk=kk, k=k, t=t: e.matmul(
                            bk[:, kk * 128:(kk + 1) * 128], lhsT=XNS[t % 2][:, k * 128:(k + 1) * 128], rhs=identb[:, :],
                            start=True, stop=True), reads=[b_xns[t % 2], b_identb], writes=[bb], signal=(kk == 3))
                    if hh == 0:
                        E("act", lambda e, bk=bk, hh=hh, t=t: e.copy(
                            out=DST[:, hh * 4:hh * 4 + 4, t * 128:(t + 1) * 128],
                            in_=bk[:, :].rearrange("p (k c) -> p k c", k=4)), reads=[bb], writes=[b_dst])
                    else:
                        E("dve", lambda e, bk=bk, hh=hh, t=t: e.tensor_copy(
                            out=DST[:, hh * 4:hh * 4 + 4, t * 128:(t + 1) * 128],
                            in_=bk[:, :].rearrange("p (k c) -> p k c", k=4)), reads=[bb], writes=[b_dst])

        blocks = [("s", 0)] + [("p", i) for i in range(4)]
        import os as _os
        KSTOP = int(_os.environ.get("KSTOP", "999"))
        bidx = {("p", 0): 0, ("p", 1): 1, ("p", 2): 2, ("p", 3): 3, ("s", 0): 4}

        def do_block(kind, bi):
            first = (kind == "s")
            nt = 4 if kind == "p" else 1
            NT = 128 * nt
            NCH = NT // 8
            xt, bx = xsel[bi % 2] if kind == "p" else xsel[0]
            XTv, bxt = XTO, b_xto
            if kind == "p":
                if bi == 1:
                    handover(b_stgs + [b_tmp], b_xb2)
                src = xp[bi * 512:(bi + 1) * 512, :].rearrange("(t p) d -> p t d", p=128)
                for t in range(4):
                    E("sp", lambda e, src=src, t=t: e.dma_start(out=xt(t), in_=src[:, t, :]), writes=[bx[t]],
                      chan=xin_ch[(bi % 2) * 4 + t])
            rmsnorm_stats_all(nt, 0, xt, bx)
            yield "stats"

            if kind == "p" and bi == 0:
                handover([b_zn], [b_xns[1]])
                for j in range(4):
                    E("dve", lambda e, j=j: e.memset(VX[:, j, 0:2], 0.0), writes=[b_vx[j]])
            norm_transpose(nt, XTv, bxt, 0, xt, bx, do_stats=False)

            def vview(j, lo):
                if kind == "p":
                    return VX[:, j, lo:lo + 512]
                return VX[:, j, 0:160].rearrange("p (b l) -> p b l", l=10)[:, :, lo:lo + 8]

            def tokview(ap2d):
                if kind == "p":
                    return ap2d
                return ap2d.rearrange("p (b l) -> p b l", l=8)

            m_order = [12, 13, 14, 15] + [m for j in range(4) for m in (j, 4 + j, 8 + j)]
            win_it = lookahead(r_win, first, [(m, (lambda t, m=m: [(
                t[:, :, :], w_in_d[:, m * 128:(m + 1) * 128].rearrange("(k p) c -> p k c", p=128), G1C[:, :])]),
                win_s[m, :, :, :]) for m in m_order], 1)
            m_seen = []

            def proj_tile(m):
                assert m == m_order[len(m_seen)]
                m_seen.append(m)
                wt, wb = next(win_it)
                bk, bb = next_bank()
                for k in range(8):
                    E("pe", lambda e, bk=bk, wt=wt, k=k: e.matmul(bk[:, 0:NT], lhsT=wt[:, k, :], rhs=XTv[:, k, 0:NT],
                                                                  start=(k == 0), stop=(k == 7)),
                      reads=[wb[k], bxt], writes=[bb], signal=(k == 7))
                return bk, bb

            if kind == "s":
                for j in range(4):
                    E("dve", lambda e, j=j: e.tensor_copy(
                        out=VX[:, j, 0:160].rearrange("p (b l) -> p b l", l=10)[:, :, 0:2],
                        in_=CCH[:, j, :, :].rearrange("p k b -> p b k")), reads=[b_cch], writes=[b_vx[j]])
            for j in range(4):
                bU, bbU = proj_tile(12 + j)
                if j % 2 == 0:
                    E("act", lambda e, bU=bU, j=j: e.copy(out=UB[:, j, 0:NT].rearrange("p (s c) -> p c s", s=8),
                                                          in_=bU[:, 0:NT].rearrange("p (c s) -> p c s", s=8)),
                      reads=[bbU], writes=[b_ub[j]])
                else:
                    E("dve", lambda e, bU=bU, j=j: e.tensor_copy(out=UB[:, j, 0:NT].rearrange("p (s c) -> p c s", s=8),
                                                                 in_=bU[:, 0:NT].rearrange("p (c s) -> p c s", s=8)),
                      reads=[bbU], writes=[b_ub[j]])

            yield "uproj"
            if kind == "s":
                load_h0()
            ebank = [next_bank(pin=True) for _ in range(4)]
            for ri in range(2):
                for j in range(4):
                    for s in range(8):
                        for k in range(4):
                            bk, bb = ebank[k]
                            lastm = (ri == 1 and j == 3 and s == 7)
                            E("pe", lambda e, bk=bk, j=j, k=k, s=s, ri=ri: e.matmul(
                                bk[:, (ri * 4 + j) * 64:(ri * 4 + j) * 64 + NCH], lhsT=BLT[32 * k:32 * k + 32, j, s, ri, :],
                                rhs=UB[32 * k:32 * k + 32, j, s * NCH:(s + 1) * NCH], start=(s == 0), stop=(s == 7),
                                tile_position=(32 * k, 0), skip_group_check=True),
                              reads=[b_blt, b_ub[j]], writes=[bb], signal=lastm)

            if kind == "p":
                for ri in range(2):
                    E("act", lambda e, ri=ri: e.copy(out=ZB[:, ri, :, 0:1], in_=ZP[:, ri, :].unsqueeze(2)),
                      reads=[b_zp], writes=[b_zb])
                def l2_group(k):
                    bke, bbe = ebank[k]
                    bre = bke[:, 0:256]
                    bim = bke[:, 256:512]
                    ks = slice(k, 16, 4)
                    v3 = lambda t: t.rearrange("p (a c) -> p a c", c=64)
                    cosv = COST[:, ks, :]
                    sinv = SINT[:, ks, :]
                    L2 = lambda fn, rd=(): E("dve", fn, reads=[b_l2, b_ssm, b_zp] + list(rd), writes=[b_l2])
                    mulop = lambda o, a_, b_, rd=(): L2(lambda e: e.tensor_tensor(out=o, in0=a_, in1=b_, op=ALU.mult), rd)
                    W0, W1, S0, S1 = v3(WIN[:, 0, :]), v3(WIN[:, 1, :]), v3(TT_[:, 0, :]), v3(TT_[:, 1, :])
                    mulop(W0, v3(bre), cosv, [bbe])
                    mulop(S0, v3(bim), sinv, [bbe])
                    mulop(W1, v3(bim), cosv, [bbe])
                    L2(lambda e: e.tensor_tensor(out=WIN[:, 0, :], in0=WIN[:, 0, :], in1=TT_[:, 0, :], op=ALU.add))
                    mulop(S1, v3(bre), sinv, [bbe])
                    L2(lambda e, ks=ks: e.tensor_tensor(out=ZI[:, 0, :], in0=ZP[:, 0, ks], in1=RR[:, ks], op=ALU.mult))
                    L2(lambda e: e.tensor_tensor(out=WIN[:, 1, :], in0=WIN[:, 1, :], in1=TT_[:, 1, :], op=ALU.subtract))
                    L2(lambda e, ks=ks: e.tensor_tensor(out=ZI[:, 1, :], in0=ZP[:, 1, ks], in1=RR[:, ks], op=ALU.mult))
                    for ri in range(2):
                        L2(lambda e, ri=ri: e.tensor_tensor(
                            out=v3(WIN[:, ri, :])[:, :, 0:1], in0=v3(WIN[:, ri, :])[:, :, 0:1],
                            in1=ZI[:, ri, :].unsqueeze(2), op=ALU.add))
                    for ri in range(2):
                        L2(lambda e, ri=ri, k=k: e.tensor_tensor_scan(
                            out=WW[:, ri, :], data0=RTAB[:, 4 * k:4 * k + 4, :].rearrange("p a c -> p (a c)"),
                            data1=WIN[:, ri, :], initial=0.0, op0=ALU.mult, op1=ALU.add))
                    mulop(W0, v3(WW[:, 0, :]), cosv)
                    mulop(S0, v3(WW[:, 1, :]), sinv)
                    mulop(W1, v3(WW[:, 0, :]), sinv)
                    mulop(S1, v3(WW[:, 1, :]), cosv)
                    L2(lambda e: e.tensor_tensor(out=WIN[:, 0, :], in0=WIN[:, 0, :], in1=TT_[:, 0, :], op=ALU.subtract))
                    L2(lambda e: e.tensor_tensor(out=WIN[:, 1, :], in0=WIN[:, 1, :], in1=TT_[:, 1, :], op=ALU.add))
                    for ri in range(2):
                        E("act", lambda e, ri=ri, ks=ks: e.copy(out=ZB[:, ri, ks, 1:65], in_=v3(WIN[:, ri, :])),
                          reads=[b_l2], writes=[b_zb])
                        E("dve", lambda e, ri=ri, ks=ks: e.tensor_copy(out=ZP[:, ri, ks], in_=v3(WIN[:, ri, :])[:, :, 63]),
                          reads=[b_l2], writes=[b_zp])
                    unpin(bbe)
            yield "headpe"
            if kind == "p":
                for i_ in range(4):
                    l2_group(i_)
                    yield "l2_%d" % i_
            yield "headdone"
            handover(b_actf, mixer_bufs)
            def conv_group(j):
                bB, bbB = proj_tile(j)
                bC, bbC = proj_tile(4 + j)
                bX, bbX = proj_tile(8 + j)
                E("act", lambda e, bX=bX: e.copy(out=XC[:, 0:NT], in_=bX[:, 0:NT]), reads=[bbX], writes=[b_xc])
                E("dve", lambda e, bC=bC, j=j: e.tensor_tensor(
                    out=vview(j, 2), in0=tokview(bC[:, 0:NT]), in1=tokview(XC[:, 0:NT]), op=ALU.mult),
                  reads=[bbC, b_xc], writes=[b_vx[j]])
                E("dve", lambda e, j=j: e.tensor_scalar(
                    out=tokview(CA[:, 0:NT]), in0=vview(j, 2), scalar1=CW[:, j, 2:3], scalar2=None, op0=ALU.mult),
                  reads=[b_vx[j], b_parm], writes=[b_ca])
                E("dve", lambda e, j=j: e.scalar_tensor_tensor(
                    out=tokview(CA[:, 0:NT]), in0=vview(j, 1), scalar=CW[:, j, 1:2], in1=tokview(CA[:, 0:NT]),
                    op0=ALU.mult, op1=ALU.add), reads=[b_vx[j], b_parm, b_ca], writes=[b_ca])
                E("dve", lambda e, j=j: e.scalar_tensor_tensor(
                    out=tokview(CA[:, 0:NT]), in0=vview(j, 0), scalar=CW[:, j, 0:1], in1=tokview(CA[:, 0:NT]),
                    op0=ALU.mult, op1=ALU.add), reads=[b_vx[j], b_parm, b_ca], writes=[b_ca])
                E("dve", lambda e, bB=bB, j=j: e.tensor_tensor(
                    out=MIX[:, j, 0:NT], in0=bB[:, 0:NT], in1=CA[:, 0:NT], op=ALU.mult),
                  reads=[bbB, b_ca], writes=[b_mix[j]])
            if first:
                prep_part3()
            if kind == "p":
                for i_ in range(4):
                    conv_group(i_)
                zsrc = lambda ri, pr: ZB[:, ri, pr, 0:64]
                b_zsrc = b_zb
            else:
                zsrc = lambda ri, pr: H0B[:, ri, pr, :]
                b_zsrc = b_h0
                for k in range(4):
                    bke, bbe = ebank[k]
                    ks = slice(k, 16, 4)
                    ev = lambda t: t.rearrange("p (a c) -> p a c", c=64)[:, :, 0:16]
                    bre = ev(bke[:, 0:256])
                    bim = ev(bke[:, 256:512])
                    lr = L8R[:, ks].unsqueeze(2).to_broadcast([128, 4, 16])
                    li = L8I[:, ks].unsqueeze(2).to_broadcast([128, 4, 16])
                    t0 = WIN[:, 0, 0:64].rearrange("p (a c) -> p a c", c=16)
                    t1 = WIN[:, 1, 0:64].rearrange("p (a c) -> p a c", c=16)
                    S2 = lambda fn, rd=(): E("dve", fn, reads=[b_l2, b_ssm, b_h0, b_zn] + list(rd), writes=[b_l2, b_zn])
                    S2(lambda e, ks=ks, lr=lr: e.tensor_tensor(out=t0, in0=H0[:, 0, ks, :], in1=lr, op=ALU.mult))
                    S2(lambda e, ks=ks, li=li: e.tensor_tensor(out=t1, in0=H0[:, 1, ks, :], in1=li, op=ALU.mult))
                    S2(lambda e: e.tensor_tensor(out=t0, in0=t0, in1=t1, op=ALU.subtract))
                    S2(lambda e, ks=ks, bre=bre: e.tensor_tensor(out=ZN[:, 0, ks, :], in0=t0, in1=bre, op=ALU.add), [bbe])
                    S2(lambda e, ks=ks, li=li: e.tensor_tensor(out=t0, in0=H0[:, 0, ks, :], in1=li, op=ALU.mult))
                    S2(lambda e, ks=ks, lr=lr: e.tensor_tensor(out=t1, in0=H0[:, 1, ks, :], in1=lr, op=ALU.mult))
                    S2(lambda e: e.tensor_tensor(out=t0, in0=t0, in1=t1, op=ALU.add))
                    S2(lambda e, ks=ks, bim=bim: e.tensor_tensor(out=ZN[:, 1, ks, :], in0=t0, in1=bim, op=ALU.add), [bbe])
                    unpin(bbe)
                for i_ in range(4):
                    conv_group(i_)

            if kind == "p" and bi == 3:
                bk, bb = next_bank()
                for j in range(4):
                    E("pe", lambda e, bk=bk, j=j: e.matmul(bk[0:2, j * 128:(j + 1) * 128], lhsT=VX[:, j, 512:514],
                                                           rhs=identf[:, :], start=True, stop=True),
                      reads=[b_vx[j], b_identf], writes=[bb])
                E("dve", lambda e, bk=bk: e.tensor_copy(out=OSC[0:2, 0:512], in_=bk[0:2, 0:512]), reads=[bb, b_osc], writes=[b_osc])
                E("pool", lambda e: e.dma_start(out=convp_o, in_=OSC[0:2, 0:512]), reads=[b_osc], chan=osc_ch)
            if kind == "p":
                for j in range(4):
                    E("dve", lambda e, j=j: e.tensor_copy(out=VX[:, j, 0:2], in_=VX[:, j, 512:514]),
                      reads=[b_vx[j]], writes=[b_vx[j]])
            if kind == "s":
                for kk in range(2):
                    bk, bb = next_bank()
                    for j in range(4):
                        E("pe", lambda e, bk=bk, j=j, kk=kk: e.matmul(
                            bk[0:16, j * 128:(j + 1) * 128],
                            lhsT=VX[:, j, 0:160].rearrange("p (b l) -> p b l", l=10)[:, :, 8 + kk],
                            rhs=identf[:, :], start=True, stop=True),
                          reads=[b_vx[j], b_identf], writes=[bb])
                    E("dve", lambda e, bk=bk, kk=kk: e.tensor_copy(out=OSC[:, kk * 512:(kk + 1) * 512], in_=bk[0:16, 0:512]),
                      reads=[bb, b_osc], writes=[b_osc])
                E("pool", lambda e: e.dma_start(out=convs_o, in_=OSC[:, :]), reads=[b_osc], chan=osc_ch)

            yield "convdone"
            ybank = []
            for j in range(4):
                bk, bb = next_bank()
                ybank.append((bk, bb))
                E("pe", lambda e, bk=bk, j=j: e.matmul(
                    bk[:, 0:NT], lhsT=KT[:, j, 0, :], rhs=UB[:, j, 0:NT], start=True, stop=False, skip_group_check=True),
                  reads=[b_kt, b_ub[j]], writes=[bb], signal=False)
                for tau in range(1, 8):
                    E("pe", lambda e, bk=bk, j=j, tau=tau: e.matmul(
                        bk[:, tau * NCH:NT], lhsT=KT[:, j, tau, :], rhs=UB[:, j, 0:(8 - tau) * NCH], start=False, stop=False,
                        skip_group_check=True), reads=[b_kt, b_ub[j]], writes=[bb], signal=False)

            for j in range(4):
                bk, bb = ybank[j]
                for s in range(8):
                    for ri in range(2):
                        for k in range(4):
                            pr = 4 * j + k
                            last = (k == 3 and s == 7 and ri == 1)
                            E("pe", lambda e, bk=bk, k=k, pr=pr, s=s, ri=ri, last=last: e.matmul(
                                bk[32 * k:32 * k + 32, s * NCH:(s + 1) * NCH], lhsT=CH[:, pr, s, ri, :], rhs=zsrc(ri, pr)[:, 0:NCH],
                                start=False, stop=last, tile_position=(0, 32 * k), skip_group_check=True),
                              reads=[b_ch, b_zsrc], writes=[bb], signal=last)
                E("act", lambda e, bk=bk, j=j: e.activation(
                    out=ZG[:, j, 0:NT].rearrange("p (c s) -> p c s", s=8),
                    in_=bk[:, 0:NT].rearrange("p (s c) -> p c s", s=8), func=AF.Gelu_apprx_tanh),
                  reads=[bb], writes=[b_zg[j]])

            if kind == "p" and bi == 3:
                bk, bb = next_bank()
                for ri in range(2):
                    E("pe", lambda e, bk=bk, ri=ri: e.matmul(bk[0:16, ri * 128:(ri + 1) * 128], lhsT=ZP[:, ri, :],
                                                             rhs=identf[:, :], start=True, stop=True),
                      reads=[b_zp, b_identf], writes=[bb])
                E("dve", lambda e, bk=bk: e.tensor_copy(out=OSC[:, 512:768], in_=bk[0:16, 0:256]), reads=[bb, b_osc], writes=[b_osc])
                E("pool", lambda e: e.dma_start(out=rep_o, in_=OSC[:, 512:640]), reads=[b_osc], chan=osc_ch)
                E("pool", lambda e: e.dma_start(out=imp_o, in_=OSC[:, 640:768]), reads=[b_osc], chan=osc_ch)
            if kind == "s":
                for ri in range(2):
                    for q in range(4):
                        bk, bb = next_bank()
                        for pp in range(4):
                            pr = q * 4 + pp
                            E("pe", lambda e, bk=bk, ri=ri, pr=pr, pp=pp: e.matmul(
                                bk[0:16, pp * 128:(pp + 1) * 128], lhsT=ZN[:, ri, pr, :], rhs=identf[:, :],
                                start=True, stop=True), reads=[b_zn, b_identf], writes=[bb])
                        E("dve", lambda e, bk=bk, q=q: e.tensor_copy(out=OST[:, q * 512:(q + 1) * 512], in_=bk[0:16, :]),
                          reads=[bb, b_xb[1], b_xb[2]], writes=[b_xb[1], b_xb[2]])
                    dst_o = res_o if ri == 0 else ims_o
                    E("pool", lambda e, dst_o=dst_o: e.dma_start(out=dst_o, in_=OST[:, :]), reads=[b_xb[1], b_xb[2]], chan=ost_ch)

            for m in range(4):
                bk, bb = next_bank()
                for k in range(4):
                    E("pe", lambda e, bk=bk, m=m, k=k: e.matmul(bk[:, 0:NT], lhsT=WGLU[:, k, m * 128:(m + 1) * 128],
                                                                rhs=ZG[:, k, 0:NT], start=(k == 0), stop=(k == 3)),
                      reads=[b_wglu] + b_zg, writes=[bb], signal=(k == 3))
                gi = m % 2
                E("act", lambda e, bk=bk, m=m, gi=gi: e.activation(out=GT[gi][:, 0:NT], in_=bk[:, 0:NT], func=AF.Sigmoid,
                                                                   bias=BG[:, m:m + 1]), reads=[bb, b_parm], writes=[b_gt[gi]])
                E("dve", lambda e, m=m, gi=gi: e.tensor_tensor(out=MIX[:, 4 + m, 0:NT], in0=ZG[:, m, 0:NT], in1=GT[gi][:, 0:NT],
                                                               op=ALU.mult), reads=[b_zg[m], b_gt[gi]], writes=[b_mix[4 + m]])

            wo_it = lookahead(r_w, False, [(("wo", k, half), (lambda t, k=k, half=half: [(
                t[:, :], wout_d[k * 128:(k + 1) * 128, half * 512:(half + 1) * 512], None)]),
                wout_s[k, :, half * 512:(half + 1) * 512]) for half in range(2) for k in range(8)], 3)
            for half in range(2):
                accs = [next_bank() for _ in range(nt)]
                for k in range(8):
                    wt, wb = next(wo_it)
                    for t in range(nt):
                        bk, bb = accs[t]
                        E("pe", lambda e, bk=bk, wt=wt, k=k, t=t: e.matmul(
                            bk[:, :], lhsT=MIX[:, k, t * 128:(t + 1) * 128], rhs=wt[:, :], start=(k == 0), stop=(k == 7)),
                          reads=[wb[0], b_mix[k]], writes=[bb], signal=(k == 7 or t == nt - 1))
                for t in range(nt):
                    bk, bb = accs[t]
                    E("dve", lambda e, bk=bk, t=t, half=half: e.tensor_tensor(
                        out=xt(t)[:, half * 512:(half + 1) * 512], in0=bk[:, :], in1=xt(t)[:, half * 512:(half + 1) * 512],
                        op=ALU.add), reads=[bb, bx[t]], writes=[bx[t]])

            yield "wodone"
            if first:
                handover([b_ssm, b_tmp, b_bd, b_cst] + b_bl, [b_ht] + b_yo + b_sil + b_gt)
            handover(mixer_bufs, b_actf)
            norm_transpose(nt, HT, b_ht, 4, xt, bx)
            yield "n2done"
            gu_it = lookahead(r_gu, first, [(f, (lambda t, f=f: [
                (t[:, 0, :, :], wg_d[:, f * 128:(f + 1) * 128].rearrange("(k p) c -> p k c", p=128), G2C[:, :]),
                (t[:, 1, :, :], wu_d[:, f * 128:(f + 1) * 128].rearrange("(k p) c -> p k c", p=128), G2C[:, :])]),
                wgu_s[f, :, :, :, :]) for f in range(NF)], 1)
            if first:
                deferred[0] = []
                prep_part2()
                p2_thunks = deferred[0]
                deferred[0] = None
            for f in range(NF):
                if first:
                    for _ in range(4):
                        if p2_thunks:
                            p2_thunks.pop(0)()
                if f in (1, 4, 7, 10):
                    yield "gu_f%d" % f
                wt, wb = next(gu_it)
                bg, bbg = next_bank()
                bu, bbu = next_bank()
                for gi, (bk, bb) in enumerate(((bg, bbg), (bu, bbu))):
                    for k in range(8):
                        E("pe", lambda e, bk=bk, wt=wt, gi=gi, k=k: e.matmul(
                            bk[:, 0:NT], lhsT=wt[:, gi, k, :], rhs=HT[:, k, 0:NT], start=(k == 0), stop=(k == 7)),
                          reads=[wb[gi * 8 + k], b_ht], writes=[bb], signal=(k == 7))
                si = f % 2
                E("act", lambda e, bg=bg, si=si: e.activation(out=SIL[si][:, 0:NT], in_=bg[:, 0:NT], func=AF.Silu),
                  reads=[bbg], writes=[b_sil[si]])
                E("dve", lambda e, bu=bu, si=si, f=f: e.tensor_tensor(
                    out=ACT_[:, f * 512:f * 512 + NT], in0=bu[:, 0:NT], in1=SIL[si][:, 0:NT], op=ALU.mult),
                  reads=[bbu, b_sil[si]], writes=[b_actf[f]])
            if first:
                while p2_thunks:
                    p2_thunks.pop(0)()
            yield "gudone"
            wd_it = lookahead(r_w, False, [(("wd", f, half), (lambda t, f=f, half=half: [(
                t[:, :], wd_d[f * 128:(f + 1) * 128, half * 512:(half + 1) * 512], None)]),
                wd_s[f, :, half * 512:(half + 1) * 512]) for half in range(2) for f in range(NF)], 3)
            for half in range(2):
                accs = [next_bank() for _ in range(nt)]
                for f in range(NF):
                    wt, wb = next(wd_it)
                    for t in range(nt):
                        bk, bb = accs[t]
                        E("pe", lambda e, bk=bk, wt=wt, f=f, t=t: e.matmul(
                            bk[:, :], lhsT=ACT_[:, f * 512 + t * 128:f * 512 + (t + 1) * 128], rhs=wt[:, :],
                            start=(f == 0), stop=(f == NF - 1)),
                          reads=[wb[0], b_actf[f]], writes=[bb], signal=(f == NF - 1 or t == nt - 1))
                for t in range(nt):
                    bk, bb = accs[t]
                    E("dve", lambda e, bk=bk, t=t, half=half: e.tensor_tensor(
                        out=xt(t)[:, half * 512:(half + 1) * 512], in0=bk[:, :], in1=xt(t)[:, half * 512:(half + 1) * 512],
                        op=ALU.add), reads=[bb, bx[t]], writes=[bx[t]])
                if half == 0:
                    yield "half0"

            rmsnorm_stats_all(nt, 8, xt, bx)
            for t in range(nt):
                yi = t % 2
                E("dve", lambda e, t=t, yi=yi: e.scalar_tensor_tensor(
                    out=YO[yi], in0=xt(t), scalar=SS[:, 8 + t:9 + t], in1=G3[:], op0=ALU.mult, op1=ALU.mult),
                  reads=[bx[t], b_ss3[2], b_G], writes=[b_yo[yi]])
                if kind == "p":
                    dst = yp[bi * 512 + t * 128: bi * 512 + (t + 1) * 128, :]
                else:
                    dst = ys
                E("pool", lambda e, dst=dst, yi=yi: e.dma_start(out=dst, in_=YO[yi]), reads=[b_yo[yi]], chan=out_ch[yi])

        def run_to(g, label):
            while True:
                got = next(g, "end")
                if got == label or got == "end":
                    assert got == label, (got, label)
                    return

        g0 = do_block("s", 0)
        run_to(g0, "uproj")
        for k in range(4):
            stage_cast(WGLU[:, k, :], wglu_d[k * 128:(k + 1) * 128, :], None, [b_wglu])
        handover([b_cst], [b_tmp])
        prep_part1()
        run_to(g0, "end")
        gens = [do_block("p", i) for i in range(4)]
        run_to(gens[0], "headdone")
        for i in range(4):
            nxt = gens[i + 1] if i + 1 < 4 else None
            run_to(gens[i], "convdone")
            if nxt is not None:
                run_to(nxt, "stats")
            run_to(gens[i], "wodone")
            if nxt is not None:
                run_to(nxt, "headpe")
            run_to(gens[i], "n2done")
            for q, f in enumerate((1, 4, 7, 10)):
                run_to(gens[i], "gu_f%d" % f)
                if nxt is not None:
                    run_to(nxt, "l2_%d" % q)
            if nxt is not None:
                run_to(nxt, "headdone")
            run_to(gens[i], "end")

        for c_ in out_ch + [osc_ch, ost_ch]:
            P.final_wait("pool", c_)
        P.replay()
    return nc


_NC_CACHE = {}


def kernel(x_prompt, x_sample, cache_conv, state_ssm_re, state_ssm_im,
           norm_mix, w_in, conv_w, ssm_lam_re, ssm_lam_im, ssm_log_dt,
           ssm_b_re, ssm_b_im, ssm_c_re, ssm_c_im, ssm_d, w_glu, b_glu, w_out,
           norm_ffn, w_gate, w_up, w_down, norm_final):
    f = lambda a: np.ascontiguousarray(np.asarray(a, dtype=np.float32))
    if "nc" not in _NC_CACHE:
        _NC_CACHE["nc"] = build_nc()
    nc = _NC_CACHE["nc"]
    shared = {
        "g1": f(norm_mix).reshape(D), "g2": f(norm_ffn).reshape(D), "g3": f(norm_final).reshape(D),
        "w_in": f(w_in).reshape(D, 2048), "convw": f(conv_w).reshape(3, 512),
        "lamr": f(ssm_lam_re).reshape(16, 128), "lami": f(ssm_lam_im).reshape(16, 128),
        "ldt": f(ssm_log_dt).reshape(16, 2),
        "br": f(ssm_b_re).reshape(32, 64, 16), "bi": f(ssm_b_im).reshape(32, 64, 16),
        "cr": f(ssm_c_re).reshape(32, 16, 64), "ci": f(ssm_c_im).reshape(32, 16, 64),
        "dsk": f(ssm_d).reshape(512), "wglu": f(w_glu).reshape(512, 512), "bglu": f(b_glu).reshape(512),
        "wout": f(w_out).reshape(D, D), "wg": f(w_gate).reshape(D, DFF), "wu": f(w_up).reshape(D, DFF),
        "wd": f(w_down).reshape(DFF, D),
    }
    xp = f(x_prompt)
    xs = f(x_sample)
    cc = f(cache_conv)
    hr = f(state_ssm_re)
    hi = f(state_ssm_im)
    in_maps = []
    for c in range(NCORES):
        m = dict(shared)
        m["xp"] = xp[c]
        m["xs"] = xs[c * 16:(c + 1) * 16].reshape(128, D)
        m["cc"] = cc[0, c * 16:(c + 1) * 16].reshape(16, 1024)
        m["h0r"] = hr[0, c * 16:(c + 1) * 16].reshape(16, 2048)
        m["h0i"] = hi[0, c * 16:(c + 1) * 16].reshape(16, 2048)
        in_maps.append(m)
    res = run_bass_kernel_spmd(nc, in_maps, core_ids=list(range(NCORES)))
    rs = res.results
    y_prompt = np.stack([r["yp"] for r in rs]).reshape(8, 2048, D)
    y_sample = np.concatenate([r["ys"].reshape(16, 8, D) for r in rs], axis=0)
    conv_p = np.stack([r["convp"] for r in rs]).reshape(1, 8, 2, 512)
    re_p = np.stack([r["rep"].reshape(32, 64) for r in rs]).reshape(1, 8, 32, 64)
    im_p = np.stack([r["imp"].reshape(32, 64) for r in rs]).reshape(1, 8, 32, 64)
    conv_s = np.concatenate([r["convs"].reshape(16, 2, 512) for r in rs], axis=0).reshape(1, 128, 2, 512)
    re_s = np.concatenate([r["res"].reshape(16, 32, 64) for r in rs], axis=0).reshape(1, 128, 32, 64)
    im_s = np.concatenate([r["ims"].reshape(16, 32, 64) for r in rs], axis=0).reshape(1, 128, 32, 64)
    return (y_prompt.astype(np.float32), y_sample.astype(np.float32), conv_p.astype(np.float32),
            re_p.astype(np.float32), im_p.astype(np.float32), conv_s.astype(np.float32),
            re_s.astype(np.float32), im_s.astype(np.float32))
```
